# Optimizing a Trainium2 kernel written in Bass

```python
import math
import jax, jax.numpy as jnp
from jax import lax
import numpy as np

D_MODEL = 2048
BATCH = 2
SEQ = 8192
DEPTH = 4

HGRN_WIDTH = D_MODEL // 2
HGRN_HEAD_DIM = 128
HGRN_HEADS = HGRN_WIDTH // HGRN_HEAD_DIM
HGRN_CHUNK = 64
ATTN_HEAD_DIM = 64
ATTN_HEADS = (D_MODEL // 2) // ATTN_HEAD_DIM
ATTN_KV_HEADS = 4
ATTN_WIDTH = ATTN_HEADS * ATTN_HEAD_DIM
KV_WIDTH = ATTN_KV_HEADS * ATTN_HEAD_DIM
WINDOW = 128
CONV_WIDTH = D_MODEL // 2
CONV_K = 3
N_BUCKETS = 32
MAX_DISTANCE = 128
ALPHA = (2.0 * DEPTH) ** 0.25
BETA = (8.0 * DEPTH) ** -0.25
LN_EPS = 1e-5
RMS_EPS = 1e-6
SPLIT_SIZES = (
    HGRN_WIDTH, HGRN_WIDTH, HGRN_WIDTH, HGRN_WIDTH,
    ATTN_WIDTH, KV_WIDTH, KV_WIDTH, ATTN_WIDTH,
    CONV_WIDTH, CONV_WIDTH, CONV_WIDTH, CONV_WIDTH,
    D_MODEL, D_MODEL, D_MODEL,
)
N_IN = sum(SPLIT_SIZES)

kernel_name = "hybrid_hgrn2_swa_sink_shortconv_gated_merge"


def layer_norm(x, g, b):
    xf = x.astype(jnp.float32)
    mu = jnp.mean(xf, axis=-1, keepdims=True)
    var = jnp.mean(jnp.square(xf - mu), axis=-1, keepdims=True)
    return ((xf - mu) * lax.rsqrt(var + LN_EPS) * g.astype(jnp.float32) + b.astype(jnp.float32)).astype(x.dtype)


def t5_bucket(dist):
    max_exact = N_BUCKETS // 2
    is_small = dist < max_exact
    logd = jnp.log(jnp.maximum(dist, 1).astype(jnp.float32) / max_exact) / math.log(MAX_DISTANCE / max_exact)
    large = max_exact + (logd * (N_BUCKETS - max_exact)).astype(jnp.int32)
    large = jnp.minimum(large, N_BUCKETS - 1)
    return jnp.where(is_small, dist, large)


def band_relative_bias(rel_bias):
    i = jnp.arange(WINDOW)[:, None]
    j = jnp.arange(2 * WINDOW)[None, :]
    rel = jnp.clip(WINDOW + i - j, 0, WINDOW - 1)
    bucket = t5_bucket(rel)
    return jnp.transpose(rel_bias[bucket], (2, 0, 1)).astype(jnp.float32)


def hgrn2_mixer(q, f_logit, inp, lb):
    bsz, t, _ = q.shape
    h, d, c = HGRN_HEADS, HGRN_HEAD_DIM, HGRN_CHUNK
    nc = t // c
    qf = jax.nn.silu(q.astype(jnp.float32)) * (d ** -0.5)
    f = lb + (1.0 - lb) * jax.nn.sigmoid(f_logit.astype(jnp.float32))
    k = 1.0 - f
    g = jnp.log(f)
    v = inp.astype(jnp.float32)

    def to_chunks(a):
        return jnp.transpose(a.reshape(bsz, nc, c, h, d), (1, 0, 3, 2, 4))

    causal = jnp.tril(jnp.ones((c, c), dtype=bool))

    def step(state, chunk):
        qc, kc, vc, gc = chunk
        b = jnp.cumsum(gc, axis=2)
        inter = jnp.einsum('bhtk,bhkv->bhtv', qc * jnp.exp(b), state)
        diff = b[:, :, :, None, :] - b[:, :, None, :, :]
        decay = jnp.exp(jnp.where(causal[None, None, :, :, None], diff, -jnp.inf))
        scores = jnp.einsum('bhtk,bhsk,bhtsk->bhts', qc, kc, decay)
        intra = jnp.einsum('bhts,bhsv->bhtv', scores, vc)
        b_end = b[:, :, -1, :]
        k_to_end = kc * jnp.exp(b_end[:, :, None, :] - b)
        new_state = jnp.exp(b_end)[..., None] * state + jnp.einsum('bhsk,bhsv->bhkv', k_to_end, vc)
        return new_state, inter + intra

    s0 = jnp.zeros((bsz, h, d, d), jnp.float32)
    _, o = lax.scan(step, s0, (to_chunks(qf), to_chunks(k), to_chunks(v), to_chunks(g)))
    return jnp.transpose(o, (1, 0, 3, 2, 4)).reshape(bsz, t, h, d)


def swa_sink_attention(q, k, v, bias, sinks):
    bsz, t, _ = q.shape
    w, dh, kvh = WINDOW, ATTN_HEAD_DIM, ATTN_KV_HEADS
    grp = ATTN_HEADS // kvh
    nb = t // w
    qb = q.reshape(bsz, nb, w, kvh, grp, dh)
    kb = k.reshape(bsz, nb, w, kvh, dh)
    vb = v.reshape(bsz, nb, w, kvh, dh)

    def with_prev(a):
        prev = jnp.concatenate([jnp.zeros_like(a[:, :1]), a[:, :-1]], axis=1)
        return jnp.concatenate([prev, a], axis=2)

    kw, vw = with_prev(kb), with_prev(vb)
    s = jnp.einsum('bnqhgd,bnkhd->bhgnqk', qb, kw).astype(jnp.float32) * (dh ** -0.5)
    s = s + bias.reshape(kvh, grp, 1, w, 2 * w)
    i = jnp.arange(w)[:, None]
    j = jnp.arange(2 * w)[None, :]
    rel = w + i - j
    band = (rel >= 0) & (rel < w)
    key_pos = jnp.arange(nb)[:, None, None] * w - w + j[None]
    mask = band[None] & (key_pos >= 0)
    s = jnp.where(mask, s, -jnp.inf)
    sink = sinks.astype(jnp.float32).reshape(kvh, grp, 1, 1, 1)
    m = jnp.maximum(jnp.max(s, axis=-1, keepdims=True), sink)
    p = jnp.exp(s - m)
    p = p / (jnp.sum(p, axis=-1, keepdims=True) + jnp.exp(sink - m))
    o = jnp.einsum('bhgnqk,bnkhd->bnqhgd', p.astype(v.dtype), vw)
    return o.reshape(bsz, t, ATTN_WIDTH)


def short_gated_conv(b_gate, c_gate, xin, conv_w):
    h = c_gate * xin
    hp = jnp.pad(h, ((0, 0), (CONV_K - 1, 0), (0, 0)))
    t = h.shape[1]
    y = conv_w[0] * hp[:, 0:t] + conv_w[1] * hp[:, 1:t + 1] + conv_w[2] * hp[:, 2:t + 2]
    return b_gate * y


def setup_inputs(seed: int = 0) -> dict:
    key = jax.random.key(seed)
    ks = jax.random.split(key, 13)
    nrm = jax.random.normal
    f32 = jnp.float32
    return {
        "x": nrm(ks[0], (BATCH, SEQ, D_MODEL), f32),
        "w_in": nrm(ks[1], (DEPTH, D_MODEL, N_IN), f32) * D_MODEL ** -0.5,
        "w_proj_hgrn": nrm(ks[2], (DEPTH, HGRN_WIDTH, D_MODEL), f32) * (HGRN_WIDTH ** -0.5 * BETA),
        "w_proj_attn": nrm(ks[3], (DEPTH, ATTN_WIDTH, D_MODEL), f32) * (ATTN_WIDTH ** -0.5 * BETA),
        "w_proj_conv": nrm(ks[4], (DEPTH, CONV_WIDTH, D_MODEL), f32) * (CONV_WIDTH ** -0.5 * BETA),
        "w_out": nrm(ks[5], (DEPTH, D_MODEL, D_MODEL), f32) * (D_MODEL ** -0.5 * BETA),
        "lb_param": nrm(ks[6], (DEPTH, HGRN_WIDTH), f32) * 0.5,
        "hgrn_norm_g": 1.0 + 0.02 * nrm(ks[7], (DEPTH, HGRN_WIDTH), f32),
        "attn_sinks": nrm(ks[8], (DEPTH, ATTN_HEADS), f32),
        "conv_w": nrm(ks[9], (DEPTH, CONV_K, CONV_WIDTH), f32) * CONV_K ** -0.5,
        "rel_bias": nrm(ks[10], (N_BUCKETS, ATTN_HEADS), f32) * 0.1,
        "ln_g": 1.0 + 0.02 * nrm(ks[11], (DEPTH, D_MODEL), f32),
        "ln_b": 0.02 * nrm(ks[12], (DEPTH, D_MODEL), f32),
    }


def reference(x, w_in, w_proj_hgrn, w_proj_attn, w_proj_conv, w_out, lb_param, hgrn_norm_g,
              attn_sinks, conv_w, rel_bias, ln_g, ln_b):
    bsz, t, _ = x.shape
    split_idx = [int(s) for s in np.cumsum(SPLIT_SIZES)[:-1]]
    lb_soft = jax.nn.softmax(lb_param.astype(jnp.float32), axis=0)
    lower_bounds = jnp.cumsum(lb_soft, axis=0) - lb_soft[0:1]
    bias = band_relative_bias(rel_bias)
    for l in range(DEPTH):
        u = x @ w_in[l]
        (a_q, a_f, a_i, a_g, b_q, b_k, b_v, b_g,
         c_b, c_c, c_x, c_g, m_a, m_b, m_c) = jnp.split(u, split_idx, axis=-1)
        o_a = hgrn2_mixer(a_q, a_f, a_i, lower_bounds[l])
        o_a = o_a * lax.rsqrt(jnp.mean(jnp.square(o_a), axis=-1, keepdims=True) + RMS_EPS)
        o_a = o_a.reshape(bsz, t, HGRN_WIDTH) * hgrn_norm_g[l].astype(jnp.float32)
        y_a = (o_a.astype(x.dtype) * jax.nn.silu(a_g)) @ w_proj_hgrn[l]
        o_b = swa_sink_attention(b_q, b_k, b_v, bias, attn_sinks[l])
        y_b = (o_b * jax.nn.silu(b_g)) @ w_proj_attn[l]
        o_c = short_gated_conv(c_b, c_c, c_x, conv_w[l])
        y_c = (o_c * jax.nn.silu(c_g)) @ w_proj_conv[l]
        merged = jax.nn.sigmoid(m_a) * y_a + jax.nn.sigmoid(m_b) * y_b + jax.nn.sigmoid(m_c) * y_c
        y = merged @ w_out[l]
        x = layer_norm(ALPHA * x + y, ln_g[l], ln_b[l])
    return x
```

```python
import numpy as np, ml_dtypes, time
from contextlib import ExitStack
import concourse.bass as bass
import concourse.mybir as mybir
from concourse.bass_utils import run_bass_kernel_spmd

F32 = mybir.dt.float32; BF16 = mybir.dt.bfloat16; I32 = mybir.dt.int32; U8 = mybir.dt.uint8
AF = mybir.ActivationFunctionType; ALU = mybir.AluOpType; AX = mybir.AxisListType
NPBF = ml_dtypes.bfloat16
ENGS = ("pe", "act", "dve", "pool", "sp")


class K:
    def __init__(self, same_sync=("act", "dve", "pool")):
        self.nc = bass.Bass("TRN2", target_bir_lowering=False)
        self.es = ExitStack()
        self.q = {e: [] for e in ENGS}
        self.cnt = {e: 0 for e in ENGS}
        self.pend = {e: False for e in ENGS}
        self.seen = {e: {} for e in ENGS}
        self.lastw = {}
        self.rd = {}
        self.dcnt = {}
        self.same_sync = set(same_sync)
        self.n_inst = 0

    def din(self, name, shape, dt):
        return self.nc.dram_tensor(name, list(shape), dt, kind="ExternalInput").ap()

    def dout(self, name, shape, dt):
        return self.nc.dram_tensor(name, list(shape), dt, kind="ExternalOutput").ap()

    def dscr(self, name, shape, dt):
        return self.nc.dram_tensor(name, list(shape), dt).ap()

    def sb(self, name, shape, dt):
        return self.es.enter_context(self.nc.sbuf_tensor("sb_" + name, list(shape), dt))

    def ps(self, name, shape, dt):
        return self.es.enter_context(self.nc.psum_tensor("ps_" + name, list(shape), dt))

    def _deps(self, E, r, w):
        deps = set()
        for t in r:
            if t in self.lastw:
                deps.add(self.lastw[t])
        for t in w:
            if t in self.lastw:
                deps.add(self.lastw[t])
            for x in self.rd.get(t, ()):
                deps.add(x)
        waits = []
        for key, val in sorted(deps, key=lambda d: (str(d[0]), d[1])):
            if key == ("eng", E) and E not in self.same_sync:
                continue
            if self.seen[E].get(key, 0) >= val:
                continue
            self.seen[E][key] = val
            waits.append((key, val))
        return waits

    def _commit(self, r, w, stamp):
        for t in w:
            self.lastw[t] = stamp
            self.rd[t] = []
        for t in r:
            self.rd.setdefault(t, []).append(stamp)

    def op(self, E, fn, r=(), w=(), inc=True):
        waits = self._deps(E, r, w)
        if inc:
            self.cnt[E] += 1
            n = self.cnt[E]
        else:
            n = self.cnt[E] + 1
        self.pend[E] = not inc
        self.q[E].append(("op", waits, fn, inc))
        self._commit(r, w, (("eng", E), n))
        self.n_inst += 1

    def dma(self, E, out, in_, sem, r=(), w=()):
        waits = self._deps(E, r, w)
        self.dcnt[sem] = self.dcnt.get(sem, 0) + 16
        self.q[E].append(("dma", waits, (out, in_), sem))
        self._commit(r, w, (("dma", sem), self.dcnt[sem]))
        self.n_inst += 1

    def finish(self):
        nc = self.nc
        for e in ENGS:
            assert not self.pend[e], f"engine {e} ends with a non-incrementing op"
        sems = {}
        for e in ENGS:
            sems[("eng", e)] = self.es.enter_context(nc.semaphore("s_" + e))
        for s in self.dcnt:
            sems[("dma", s)] = self.es.enter_context(nc.semaphore("d_" + s))
        block = self.es.enter_context(nc.Block())
        final = [(("dma", s), v) for s, v in self.dcnt.items()]

        def replay(E, eng):
            for item in self.q[E]:
                kind, waits = item[0], item[1]
                for key, val in waits:
                    eng.wait_ge(sems[key], val)
                if kind == "op":
                    ins = item[2](eng)
                    if item[3]:
                        ins.then_inc(sems[("eng", E)], 1)
                else:
                    out, in_ = item[2]
                    eng.dma_start(out=out, in_=in_).then_inc(sems[("dma", item[3])], 16)
            if E == "sp":
                for key, val in final:
                    eng.wait_ge(sems[key], val)

        @block.tensor
        def _(eng):
            replay("pe", eng)

        @block.scalar
        def _(eng):
            replay("act", eng)

        @block.vector
        def _(eng):
            replay("dve", eng)

        @block.gpsimd
        def _(eng):
            replay("pool", eng)

        @block.sync
        def _(eng):
            replay("sp", eng)

        self.es.close()
        return nc


D = 2048; KC = 16
C_AQ, C_AF, C_AI, C_AG = 0, 8, 16, 24
C_BQ, C_BK, C_BV, C_BG = 32, 40, 42, 44
C_CB, C_CC, C_CX, C_CG = 52, 60, 68, 76
C_MA, C_MB, C_MC = 84, 100, 116
RMS_EPS = 1e-6; LN_EPS = 1e-5
ALPHA = (2.0 * 4) ** 0.25
NEG = -30000.0


class Ctx:
    def __init__(self, k, T, nslab=3, npsum=8, load_xT=True, slabs=None):
        self.k = k; self.T = T
        self.slabs = list(range(132)) if slabs is None else list(slabs)
        self.xT_d = k.din("xT", [16, 128, T], BF16)
        self.win = k.din("w_in_t", [len(self.slabs), 128, 16 * 128], F32)
        self.xTs = k.sb("xTs", [128, 16, T if load_xT else 8], BF16)
        self.wsl = [k.sb(f"wsl{i}", [128, 16, 128], BF16) for i in range(nslab)]
        self.nsl = 0
        self.pb = [k.ps(f"pb{i}", [128, 512], F32) for i in range(npsum)]
        self.npb = 0
        for kc in range(16 if load_xT else 0):
            k.dma("sp", self.xTs[:, kc, :], self.xT_d[kc, :, :], "xT", w=[("xT", kc)])
        for kc in range(16 if load_xT else 0):
            k.lastw[("xT", kc)] = (("dma", "xT"), k.dcnt["xT"])
        self.xT_tok = [("xT", kc) for kc in range(16)]

    def slab(self, j, src=None):
        k = self.k
        s = self.nsl % len(self.wsl); self.nsl += 1
        tok = f"wsl{s}"
        srcap = self.win[self.slabs.index(j), :, :] if src is None else src[j, :, :]
        k.dma("pool", self.wsl[s][:].rearrange("p a b -> p (a b)"), srcap, tok, w=[tok])
        return self.wsl[s], tok

    def bank(self):
        b = self.npb % len(self.pb); self.npb += 1
        return self.pb[b], f"pb{b}"

    def fm(self, wt, wtok, t0, n, c0=0, c1=128):
        k = self.k
        P, ptok = self.bank()
        for kc in range(16):
            k.op("pe", lambda e, P=P, wt=wt, kc=kc, t0=t0, n=n: e.matmul(P[0:c1 - c0, 0:n], wt[:, kc, c0:c1], self.xTs[:, kc, t0:t0 + n], start=(kc == 0), stop=(kc == 15)),
                 r=[wtok, ("xT", kc)], w=[ptok], inc=(kc == 15))
        return P, ptok

    def tm(self, wt, wtok, t0, m, P, ptok, c0, first=True):
        k = self.k
        for kc in range(16):
            k.op("pe", lambda e, P=P, wt=wt, kc=kc, t0=t0, m=m, c0=c0: e.matmul(P[0:m, c0:c0 + 128], self.xTs[:, kc, t0:t0 + m], wt[:, kc, :], start=(kc == 0), stop=(kc == 15)),
                 r=[wtok, ("xT", kc)], w=[ptok], inc=(kc == 15))


def lower_bound(k, c):
    lbn_d = k.din("lb_num", [128, 8, 4], F32); lbd_d = k.din("lb_den", [128, 8, 4], F32)
    lbn = k.sb("lbn", [128, 8, 4], F32); lbd = k.sb("lbd", [128, 8, 4], F32)
    lbs = k.sb("lbs", [128, 4, 8], F32)
    k.dma("sp", lbn[:], lbn_d[:, :, :], "lbn", w=["lbn"])
    k.dma("sp", lbd[:], lbd_d[:, :, :], "lbd", w=["lbd"])
    k.op("act", lambda e: e.activation(out=lbd[:], in_=lbd[:], func=AF.Exp), r=["lbd"], w=["lbd"])
    k.op("dve", lambda e: e.tensor_tensor(out=lbn[:], in0=lbn[:], in1=lbd[:], op=ALU.mult), r=["lbn", "lbd"], w=["lbn"])
    k.op("dve", lambda e: e.tensor_reduce(out=lbs[:, 0, :], in_=lbn[:], axis=AX.X, op=ALU.add), r=["lbn"], w=["lbs"])
    k.op("dve", lambda e: e.tensor_reduce(out=lbs[:, 1, :], in_=lbd[:], axis=AX.X, op=ALU.add), r=["lbd"], w=["lbs"])
    k.op("dve", lambda e: e.reciprocal(out=lbs[:, 1, :], in_=lbs[:, 1, :]), r=["lbs"], w=["lbs"])
    k.op("dve", lambda e: e.tensor_tensor(out=lbs[:, 2, :], in0=lbs[:, 0, :], in1=lbs[:, 1, :], op=ALU.mult), r=["lbs"], w=["lbs"])
    k.op("dve", lambda e: e.tensor_scalar(out=lbs[:, 3, :], in0=lbs[:, 2, :], scalar1=-1.0, scalar2=1.0, op0=ALU.mult, op1=ALU.add), r=["lbs"], w=["lbs"])
    return lbs


def forget_gate(k, c, h, lbs, fbuf, ftok):
    T = c.T
    wt, wtok = c.slab(C_AF + h)
    for tg in range(T // 512 if T >= 512 else 1):
        n = min(512, T)
        P, ptok = c.fm(wt, wtok, tg * 512, n)
        k.op("act", lambda e, P=P, tg=tg, n=n: e.activation(out=fbuf[:, tg * 512:tg * 512 + n], in_=P[:, 0:n], func=AF.Sigmoid), r=[ptok], w=[ftok])
    k.op("dve", lambda e: e.tensor_scalar(out=fbuf[:], in0=fbuf[:], scalar1=lbs[:, 3, h:h + 1], scalar2=lbs[:, 2, h:h + 1], op0=ALU.mult, op1=ALU.add), r=[ftok, "lbs"], w=[ftok])


SL_P1 = list(range(8, 24)) + [42, 43] + list(range(60, 76))
SL_A = list(range(0, 32))
SL_B = list(range(32, 52))
SL_C = list(range(52, 84))
SL_D = list(range(84, 132))


def build_p1(T, full=True):
    k = K(); c = Ctx(k, T, npsum=7, slabs=None if full else SL_P1)
    NB = T // 128
    ident = k.din("ident", [128, 128], BF16)
    wkdup = k.din("wk_dup_t", [4, 128, 16 * 128], F32)
    o_sloc = k.dout("s_loc", [8, 128, 128], F32)
    o_dseg = k.dout("d_seg", [128, 8], F32)
    o_kh = k.dout("kT_halo", [4, 128, 128], BF16)
    o_vh = k.dout("v_halo", [128, 256], BF16)
    o_hh = k.dout("h_halo", [8, 128, 2], F32)
    idb = k.sb("idb", [128, 128], BF16)
    k.dma("sp", idb[:], ident[:, :], "id", w=["idb"])
    lbs = lower_bound(k, c)
    ones = k.sb("ones", [128, T], F32)
    k.op("pool", lambda e: e.memset(ones[:], 1.0), w=["ones"])
    fb = k.sb("fb", [128, T], F32); gb = k.sb("gb", [128, T], F32); bs = k.sb("bs", [128, T], F32)
    ksg = k.sb("ksg", [128, T], BF16)
    vh = k.sb("vh", [128, NB, 128], BF16)
    ksT = k.sb("ksT", [128, NB, 128], BF16)
    dsg = k.sb("dsg", [128, 8], F32)
    sst = [k.sb(f"sst{i}", [128, 128], F32) for i in range(2)]
    ptr = [k.ps(f"ptr{i}", [128, 4, 128], BF16) for i in range(1)]
    for h in range(8):
        forget_gate(k, c, h, lbs, fb, "fb")
        k.op("act", lambda e: e.activation(out=gb[:], in_=fb[:], func=AF.Ln), r=["fb"], w=["gb"])
        k.op("dve", lambda e: e.tensor_tensor_scan(out=bs[:], data0=ones[:], data1=gb[:], initial=0.0, op0=ALU.mult, op1=ALU.add), r=["ones", "gb"], w=["bs"])
        k.op("act", lambda e, h=h: e.activation(out=dsg[:, h:h + 1], in_=bs[:, T - 1:T], func=AF.Exp), r=["bs"], w=["dsg"])
        k.op("dve", lambda e: e.tensor_scalar(out=gb[:], in0=bs[:], scalar1=-1.0, scalar2=bs[:, T - 1:T], op0=ALU.mult, op1=ALU.add), r=["bs"], w=["gb"])
        k.op("act", lambda e: e.activation(out=gb[:], in_=gb[:], func=AF.Exp), r=["gb"], w=["gb"])
        k.op("dve", lambda e: e.tensor_scalar(out=fb[:], in0=fb[:], scalar1=-1.0, scalar2=1.0, op0=ALU.mult, op1=ALU.add), r=["fb"], w=["fb"])
        k.op("dve", lambda e: e.tensor_tensor(out=ksg[:], in0=fb[:], in1=gb[:], op=ALU.mult), r=["fb", "gb"], w=["ksg"])
        wt, wtok = c.slab(C_AI + h)
        for g4 in range(NB // 4 if NB >= 4 else 1):
            nb4 = min(4, NB)
            P, ptok = c.bank()
            for j in range(nb4):
                c.tm(wt, wtok, (g4 * 4 + j) * 128, 128, P, ptok, j * 128)
            k.op("act", lambda e, P=P, g4=g4, nb4=nb4: e.copy(out=vh[:, g4 * 4:g4 * 4 + nb4, :], in_=P[:, 0:nb4 * 128]), r=[ptok], w=["vh"])
        for g4 in range(NB // 4 if NB >= 4 else 1):
            nb4 = min(4, NB)
            for j in range(nb4):
                tb = g4 * 4 + j
                k.op("pe", lambda e, j=j, tb=tb: e.transpose(ptr[0][:, j, :], ksg[:, tb * 128:(tb + 1) * 128], idb[:]), r=["ksg", "idb"], w=["ptr0"], inc=(j == nb4 - 1))
            k.op("dve", lambda e, g4=g4, nb4=nb4: e.tensor_copy(out=ksT[:, g4 * 4:g4 * 4 + nb4, :], in_=ptr[0][:, 0:nb4, :]), r=["ptr0"], w=["ksT"])
        P, ptok = c.bank()
        for tb in range(NB):
            k.op("pe", lambda e, P=P, tb=tb: e.matmul(P[:, 0:128], ksT[:, tb, :], vh[:, tb, :], start=(tb == 0), stop=(tb == NB - 1)), r=["ksT", "vh"], w=[ptok], inc=(tb == NB - 1))
        s = h % 2
        k.op("act", lambda e, P=P, s=s: e.copy(out=sst[s][:], in_=P[:, 0:128]), r=[ptok], w=[f"sst{s}"])
        k.dma("sp", o_sloc[h, :, :], sst[s][:], f"sst{s}", r=[f"sst{s}"])
    k.dma("sp", o_dseg[:, :], dsg[:], "dsg", r=["dsg"])
    t0 = T - 128
    kst = k.sb("kst", [128, 4, 128], BF16)
    for kv in range(4):
        wt, wtok = c.slab(kv, src=wkdup)
        P, ptok = c.fm(wt, wtok, t0, 128)
        k.op("act", lambda e, P=P, kv=kv: e.copy(out=kst[:, kv, :], in_=P[:, 0:128]), r=[ptok], w=["kst"])
    k.dma("sp", o_kh.rearrange("a p t -> p a t"), kst[:], "kst", r=["kst"])
    vst = k.sb("vst", [128, 256], BF16)
    P, ptok = c.bank()
    for j in range(2):
        wt, wtok = c.slab(C_BV + j)
        c.tm(wt, wtok, t0, 128, P, ptok, j * 128)
    k.op("act", lambda e, P=P: e.copy(out=vst[:], in_=P[:, 0:256]), r=[ptok], w=["vst"])
    k.dma("sp", o_vh[:, :], vst[:], "vst", r=["vst"])
    hst = k.sb("hst", [128, 8, 2], F32)
    ctmp = k.sb("ctmp", [128, 2], F32)
    for j in range(8):
        wt, wtok = c.slab(C_CC + j)
        P1, p1tok = c.fm(wt, wtok, t0, 128)
        wt2, wtok2 = c.slab(C_CX + j)
        P2, p2tok = c.fm(wt2, wtok2, t0, 128)
        k.op("act", lambda e, P1=P1: e.copy(out=ctmp[:], in_=P1[:, 126:128]), r=[p1tok], w=["ctmp"])
        k.op("dve", lambda e, P2=P2, j=j: e.tensor_tensor(out=hst[:, j, :], in0=ctmp[:], in1=P2[:, 126:128], op=ALU.mult), r=["ctmp", p2tok], w=["hst"])
    k.dma("sp", o_hh.rearrange("a p t -> p a t"), hst[:], "hst", r=["hst"])
    return k.finish()


def build_a(T, full=True):
    k = K(); c = Ctx(k, T, npsum=3, slabs=None if full else SL_A)
    NCH = T // 64; TG = min(512, T); NG = T // TG; CPG = TG // 64
    ident = k.din("ident", [128, 128], BF16)
    tri_d = k.din("tri", [64, 64], I32)
    sprev_d = k.din("s_prev", [3, 8, 128, 128], F32)
    dprev_d = k.din("d_prev", [3, 128, 8], F32)
    ng_d = k.din("norm_g", [128, 8], F32)
    o_a = k.dout("o_a", [8, 128, T], BF16)
    idb = k.sb("idb", [128, 128], BF16); tri = k.sb("tri", [64, 64], I32)
    k.dma("sp", idb[:], ident[:, :], "id", w=["idb"])
    k.dma("sp", tri[:], tri_d[:, :], "tri", w=["tri"])
    sprev = k.sb("sprev", [128, 3, 8, 128], F32); dprev = k.sb("dprev", [128, 3, 8], F32); ng = k.sb("ng", [128, 8], F32)
    for j in range(3):
        k.dma("sp", sprev[:, j, :, :], sprev_d[j].rearrange("h k v -> k h v"), "sprev", w=["sprev"])
        k.dma("sp", dprev[:, j, :], dprev_d[j, :, :], "dprev", w=["dprev"])
    k.dma("sp", ng[:], ng_d[:, :], "ng", w=["ng"])
    lbs = lower_bound(k, c)
    m01 = k.sb("m01", [128, T], F32)
    k.op("pool", lambda e: e.memset(m01[:], 1.0), w=["m01"])
    k.op("pool", lambda e: e.memset(m01[:].rearrange("p (c s) -> p c s", s=64)[:, :, 0:1], 0.0), r=["m01"], w=["m01"])
    onesb = k.sb("onesb", [128, 128], BF16)
    k.op("pool", lambda e: e.memset(onesb[:], 1.0), w=["onesb"])
    Sf = k.sb("Sf", [128, 8, 128], F32)
    k.op("dve", lambda e: e.tensor_copy(out=Sf[:], in_=sprev[:, 0, :, :]), r=["sprev"], w=["Sf"])
    for j in (1, 2):
        for h in range(8):
            k.op("dve", lambda e, j=j, h=h: e.scalar_tensor_tensor(out=Sf[:, h, :], in0=Sf[:, h, :], scalar=dprev[:, j, h:h + 1], in1=sprev[:, j, h, :], op0=ALU.mult, op1=ALU.add),
                 r=["Sf", "dprev", "sprev"], w=["Sf"])
    qs = k.sb("qs", [128, T], F32); fb = k.sb("fb", [128, T], F32); bb = k.sb("bb", [128, T], F32)
    t1 = k.sb("t1", [128, T], F32); t2 = k.sb("t2", [128, T], F32)
    qt = k.sb("qt", [128, T], BF16); qe = k.sb("qe", [128, T], BF16); kt = k.sb("kt", [128, T], BF16); kte = k.sb("kte", [128, T], BF16)
    dec = k.sb("dec", [128, NCH], F32)
    vh = k.sb("vh", [64, NCH, 128], BF16); kteT = k.sb("kteT", [64, NCH, 128], BF16)
    Sb = [k.sb(f"Sb{i}", [128, 128], BF16) for i in range(2)]
    ptm = [k.sb(f"ptm{i}", [64, 64], BF16) for i in range(2)]
    osq = k.sb("osq", [128, TG], BF16); rs = k.sb("rs", [128, TG], F32)
    oast = [k.sb(f"oast{i}", [128, T], BF16) for i in range(2)]
    po = [k.ps(f"po{i}", [128, 512], F32) for i in range(2)]
    pss = k.ps("pss", [128, 512], F32)
    pmisc = k.ps("pmisc", [128, 512], F32)
    ptr = k.ps("ptr", [64, 4, 128], BF16)
    for i in range(2):
        k.op("pool", lambda e, i=i: e.memset(ptm[i][:], 0.0), w=[f"ptm{i}"])
    b3 = lambda ap: ap.rearrange("p (c s) -> p c s", s=64)
    nsb = 0; npt = 0
    for h in range(8):
        wt, wtok = c.slab(C_AQ + h)
        for tg in range(NG):
            P, ptok = c.fm(wt, wtok, tg * TG, TG)
            k.op("act", lambda e, P=P, tg=tg: e.activation(out=qs[:, tg * TG:(tg + 1) * TG], in_=P[:, 0:TG], func=AF.Silu), r=[ptok], w=["qs"])
        forget_gate(k, c, h, lbs, fb, "fb")
        k.op("act", lambda e: e.activation(out=t1[:], in_=fb[:], func=AF.Ln), r=["fb"], w=["t1"])
        k.op("dve", lambda e: e.tensor_tensor_scan(out=bb[:], data0=m01[:], data1=t1[:], initial=0.0, op0=ALU.mult, op1=ALU.add), r=["m01", "t1"], w=["bb"])
        k.op("dve", lambda e: e.tensor_scalar(out=fb[:], in0=fb[:], scalar1=-1.0, scalar2=1.0, op0=ALU.mult, op1=ALU.add), r=["fb"], w=["fb"])
        k.op("act", lambda e: e.activation(out=dec[:], in_=b3(bb[:])[:, :, 63], func=AF.Exp), r=["bb"], w=["dec"])
        k.op("dve", lambda e: e.tensor_tensor(out=b3(t1[:]), in0=b3(bb[:]), in1=b3(bb[:])[:, :, 31:32].broadcast_to([128, NCH, 64]), op=ALU.subtract), r=["bb"], w=["t1"])
        k.op("act", lambda e: e.activation(out=t2[:], in_=t1[:], func=AF.Exp), r=["t1"], w=["t2"])
        k.op("dve", lambda e: e.scalar_tensor_tensor(out=qt[:], in0=qs[:], scalar=128 ** -0.5, in1=t2[:], op0=ALU.mult, op1=ALU.mult), r=["qs", "t2"], w=["qt"])
        k.op("act", lambda e: e.activation(out=t2[:], in_=t1[:], func=AF.Exp, scale=-1.0), r=["t1"], w=["t2"])
        k.op("dve", lambda e: e.tensor_tensor(out=kt[:], in0=fb[:], in1=t2[:], op=ALU.mult), r=["fb", "t2"], w=["kt"])
        k.op("dve", lambda e: e.tensor_tensor(out=b3(t1[:]), in0=b3(bb[:])[:, :, 63:64].broadcast_to([128, NCH, 64]), in1=b3(bb[:]), op=ALU.subtract), r=["bb"], w=["t1"])
        k.op("act", lambda e: e.activation(out=t2[:], in_=t1[:], func=AF.Exp), r=["t1"], w=["t2"])
        k.op("dve", lambda e: e.tensor_tensor(out=kte[:], in0=fb[:], in1=t2[:], op=ALU.mult), r=["fb", "t2"], w=["kte"])
        k.op("act", lambda e: e.activation(out=t2[:], in_=bb[:], func=AF.Exp), r=["bb"], w=["t2"])
        k.op("dve", lambda e: e.scalar_tensor_tensor(out=qe[:], in0=qs[:], scalar=128 ** -0.5, in1=t2[:], op0=ALU.mult, op1=ALU.mult), r=["qs", "t2"], w=["qe"])
        wt, wtok = c.slab(C_AI + h)
        for c4 in range(NCH // 4):
            P, ptok = c.bank()
            for j in range(4):
                c.tm(wt, wtok, (c4 * 4 + j) * 64, 64, P, ptok, j * 128)
            k.op("act", lambda e, P=P, c4=c4: e.copy(out=vh[:, c4 * 4:c4 * 4 + 4, :], in_=P[0:64, :]), r=[ptok], w=["vh"])
        for c4 in range(NCH // 4):
            for j in range(4):
                cc = c4 * 4 + j
                k.op("pe", lambda e, j=j, cc=cc: e.transpose(ptr[:, j, :], kte[:, cc * 64:(cc + 1) * 64], idb[:]), r=["kte", "idb"], w=["ptr"], inc=(j == 3))
            k.op("dve", lambda e, c4=c4: e.tensor_copy(out=kteT[:, c4 * 4:c4 * 4 + 4, :], in_=ptr[:]), r=["ptr"], w=["kteT"])
        sb_cur = nsb % 2; nsb += 1
        k.op("act", lambda e, h=h, s=sb_cur: e.copy(out=Sb[s][:], in_=Sf[:, h, :]), r=["Sf"], w=[f"Sb{sb_cur}"])
        for cc in range(NCH):
            g = cc // CPG; j = cc % CPG; pg = g % 2
            s = npt % 2; npt += 1
            cs = slice(cc * 64, (cc + 1) * 64)
            k.op("pe", lambda e, s=s, cs=cs: e.matmul(pmisc[0:64, 256 + s * 64:320 + s * 64], kt[:, cs], qt[:, cs], start=True, stop=True), r=["kt", "qt"], w=[f"ppt{s}"])
            k.op("dve", lambda e, s=s: e.copy_predicated(out=ptm[s][:], mask=tri[:], data=pmisc[0:64, 256 + s * 64:320 + s * 64]), r=[f"ppt{s}", "tri"], w=[f"ptm{s}"])
            k.op("pe", lambda e, s=s, cc=cc, pg=pg, j=j: e.matmul(po[pg][:, j * 64:(j + 1) * 64], vh[:, cc, :], ptm[s][:], start=True, stop=False), r=["vh", f"ptm{s}"], w=[f"po{pg}"], inc=False)
            k.op("pe", lambda e, sb_cur=sb_cur, cs=cs, pg=pg, j=j: e.matmul(po[pg][:, j * 64:(j + 1) * 64], Sb[sb_cur][:], qe[:, cs], start=False, stop=True), r=[f"Sb{sb_cur}", "qe"], w=[f"po{pg}"])
            k.op("pe", lambda e, s=s, cc=cc: e.matmul(pmisc[:, s * 128:(s + 1) * 128], kteT[:, cc, :], vh[:, cc, :], start=True, stop=True), r=["kteT", "vh"], w=[f"pkv{s}"])
            k.op("dve", lambda e, h=h, cc=cc, s=s: e.scalar_tensor_tensor(out=Sf[:, h, :], in0=Sf[:, h, :], scalar=dec[:, cc:cc + 1], in1=pmisc[:, s * 128:(s + 1) * 128], op0=ALU.mult, op1=ALU.add), r=["Sf", "dec", f"pkv{s}"], w=["Sf"])
            sb_cur = nsb % 2; nsb += 1
            k.op("act", lambda e, h=h, s2=sb_cur: e.copy(out=Sb[s2][:], in_=Sf[:, h, :]), r=["Sf"], w=[f"Sb{sb_cur}"])
            if j == CPG - 1:
                gs = slice(g * TG, (g + 1) * TG)
                k.op("act", lambda e, pg=pg: e.activation(out=osq[:], in_=po[pg][:, 0:TG], func=AF.Square), r=[f"po{pg}"], w=["osq"])
                k.op("pe", lambda e: e.matmul(pss[:, 0:TG], onesb[:], osq[:], start=True, stop=True), r=["onesb", "osq"], w=["pss"])
                k.op("act", lambda e: e.activation(out=rs[:], in_=pss[:, 0:TG], func=AF.Sqrt, scale=1.0 / 128, bias=RMS_EPS), r=["pss"], w=["rs"])
                k.op("dve", lambda e: e.reciprocal(out=rs[:], in_=rs[:]), r=["rs"], w=["rs"])
                k.op("dve", lambda e, pg=pg, h=h, gs=gs: e.scalar_tensor_tensor(out=t1[:, gs], in0=po[pg][:, 0:TG], scalar=ng[:, h:h + 1], in1=rs[:], op0=ALU.mult, op1=ALU.mult), r=[f"po{pg}", "ng", "rs"], w=["t1"])
        wt, wtok = c.slab(C_AG + h)
        os_ = h % 2
        for tg in range(NG):
            gs = slice(tg * TG, (tg + 1) * TG)
            P, ptok = c.fm(wt, wtok, tg * TG, TG)
            k.op("act", lambda e, P=P, gs=gs: e.activation(out=t2[:, gs], in_=P[:, 0:TG], func=AF.Silu), r=[ptok], w=["t2"])
            k.op("pool", lambda e, gs=gs, os_=os_: e.tensor_tensor(out=oast[os_][:, gs], in0=t1[:, gs], in1=t2[:, gs], op=ALU.mult), r=["t1", "t2"], w=[f"oast{os_}"])
        k.dma("sp", o_a[h, :, :], oast[os_][:], f"oast{os_}", r=[f"oast{os_}"])
    o_S = k.dout("S_end", [128, 8, 128], F32)
    k.dma("sp", o_S[:, :, :], Sf[:], "Sf_out", r=["Sf"])
    return k.finish()


def build_b(T, full=True):
    k = K(); c = Ctx(k, T, npsum=3, slabs=None if full else SL_B)
    NB = T // 128; TG = min(512, T); NG = T // TG
    ident = k.din("ident", [128, 128], BF16)
    kh_d = k.din("kT_halo", [4, 128, 128], BF16)
    vh_d = k.din("v_halo", [128, 256], BF16)
    bias_d = k.din("bias", [16, 128, 256], F32)
    nm_d = k.din("negmask", [128, 256], F32)
    fn_d = k.din("firstneg", [128, 1], F32)
    sk_d = k.din("sinks", [128, 16], F32)
    o_b = k.dout("o_b", [8, 128, T], BF16)
    idb = k.sb("idb", [128, 128], BF16)
    k.dma("sp", idb[:], ident[:, :], "id", w=["idb"])
    bm = k.sb("bm", [128, 16, 256], F32); nm = k.sb("nm", [128, 256], F32); fneg = k.sb("fneg", [128, 1], F32); sinkb = k.sb("sinkb", [128, 16], F32)
    for hq in range(16):
        k.dma("sp", bm[:, hq, :], bias_d[hq, :, :], "bm", w=["bm"])
    k.dma("sp", nm[:], nm_d[:, :], "nm", w=["nm"])
    k.dma("sp", fneg[:], fn_d[:, :], "fneg", w=["fneg"])
    k.dma("sp", sinkb[:], sk_d[:, :], "sinkb", w=["sinkb"])
    k.op("dve", lambda e: e.tensor_tensor(out=bm[:], in0=bm[:], in1=nm[:].unsqueeze(1).broadcast_to([128, 16, 256]), op=ALU.add), r=["bm", "nm"], w=["bm"])
    kT2 = k.sb("kT2", [64, 4, 128 + T], BF16)
    vtok = k.sb("vtok", [128, NB + 1, 256], BF16)
    k.dma("sp", kT2[:, :, 0:128], kh_d[:, 0:64, :].rearrange("a p t -> p a t"), "kT2h", w=["kT2h"])
    k.dma("sp", vtok[:, 0, :], vh_d[:, :], "vtokh", w=["vtokh"])
    for kv in range(4):
        if kv % 2 == 0:
            wt, wtok = c.slab(C_BK + kv // 2)
        for tg in range(NG):
            P, ptok = c.fm(wt, wtok, tg * TG, TG, (kv % 2) * 64, (kv % 2) * 64 + 64)
            k.op("act", lambda e, P=P, kv=kv, tg=tg: e.copy(out=kT2[:, kv, 128 + tg * TG:128 + (tg + 1) * TG], in_=P[0:64, 0:TG]), r=[ptok], w=["kT2"])
    wv = [c.slab(C_BV + j) for j in range(2)]
    for t2 in range(NB // 2):
        P, ptok = c.bank()
        for blk in range(2):
            for j in range(2):
                c.tm(wv[j][0], wv[j][1], (t2 * 2 + blk) * 128, 128, P, ptok, blk * 256 + j * 128)
        k.op("act", lambda e, P=P, t2=t2: e.copy(out=vtok[:, 1 + t2 * 2:3 + t2 * 2, :], in_=P[:, :]), r=[ptok], w=["vtok"])
    qT = k.sb("qT", [64, 2, T], BF16); gate = k.sb("gate", [128, T], F32)
    obT = [k.sb(f"obT{i}", [128, T], BF16) for i in range(2)]
    sc = [k.sb(f"sc{i}", [128, 2, 256], F32) for i in range(2)]
    pp = [k.sb(f"pp{i}", [128, 2, 256], BF16) for i in range(2)]
    pT = [k.sb(f"pT{i}", [128, 4, 128], BF16) for i in range(2)]
    on = [k.sb(f"on{i}", [128, 2, 64], BF16) for i in range(2)]
    st = [k.sb(f"st{i}", [128, 8, 2], F32) for i in range(2)]
    psS = [k.ps(f"psS{i}", [128, 2, 256], F32) for i in range(2)]
    ppT = k.ps("ppT", [128, 4, 128], BF16)
    po = k.ps("po", [128, 2, 64], F32)
    poT = k.ps("poT", [128, 128], BF16)
    it = 0
    for qc in range(8):
        kvh = qc // 2
        wt, wtok = c.slab(C_BQ + qc)
        for tg in range(NG):
            for hh in range(2):
                P, ptok = c.fm(wt, wtok, tg * TG, TG, hh * 64, hh * 64 + 64)
                k.op("act", lambda e, P=P, tg=tg, hh=hh: e.mul(out=qT[:, hh, tg * TG:(tg + 1) * TG], in_=P[0:64, 0:TG], mul=0.125), r=[ptok], w=["qT"])
        wt, wtok = c.slab(C_BG + qc)
        for tg in range(NG):
            P, ptok = c.fm(wt, wtok, tg * TG, TG)
            k.op("act", lambda e, P=P, tg=tg: e.activation(out=gate[:, tg * TG:(tg + 1) * TG], in_=P[:, 0:TG], func=AF.Silu), r=[ptok], w=["gate"])
        ost = qc % 2
        for n in range(NB):
            s = it % 2; it += 1
            S_, sc_, pp_, pT_, on_, st_ = psS[s], sc[s], pp[s], pT[s], on[s], st[s]
            for hh in range(2):
                k.op("pe", lambda e, S_=S_, hh=hh, n=n, kvh=kvh: e.matmul(S_[:, hh, :], qT[:, hh, n * 128:(n + 1) * 128], kT2[:, kvh, n * 128:n * 128 + 256], start=True, stop=True),
                     r=["qT", "kT2", "kT2h"], w=[f"psS{s}"], inc=(hh == 1))
            k.op("dve", lambda e, S_=S_, sc_=sc_, qc=qc: e.tensor_tensor(out=sc_[:], in0=S_[:], in1=bm[:, 2 * qc:2 * qc + 2, :], op=ALU.add), r=[f"psS{s}", "bm"], w=[f"sc{s}"])
            if n == 0:
                k.op("dve", lambda e, sc_=sc_: e.tensor_scalar(out=sc_[:, :, 0:128], in0=sc_[:, :, 0:128], scalar1=fneg[:, 0:1], scalar2=None, op0=ALU.add), r=[f"sc{s}", "fneg"], w=[f"sc{s}"])
            k.op("dve", lambda e, sc_=sc_, st_=st_: e.tensor_reduce(out=st_[:, 0, :], in_=sc_[:], axis=AX.X, op=ALU.max), r=[f"sc{s}"], w=[f"st{s}"])
            k.op("dve", lambda e, st_=st_, qc=qc: e.tensor_tensor(out=st_[:, 0, :], in0=st_[:, 0, :], in1=sinkb[:, 2 * qc:2 * qc + 2], op=ALU.max), r=[f"st{s}", "sinkb"], w=[f"st{s}"])
            k.op("dve", lambda e, st_=st_: e.tensor_scalar(out=st_[:, 1, :], in0=st_[:, 0, :], scalar1=-1.0, scalar2=None, op0=ALU.mult), r=[f"st{s}"], w=[f"st{s}"])
            k.op("dve", lambda e, st_=st_, qc=qc: e.tensor_tensor(out=st_[:, 3, :], in0=st_[:, 1, :], in1=sinkb[:, 2 * qc:2 * qc + 2], op=ALU.add), r=[f"st{s}", "sinkb"], w=[f"st{s}"])
            for hh in range(2):
                k.op("act", lambda e, sc_=sc_, pp_=pp_, st_=st_, hh=hh: e.activation(out=pp_[:, hh, :], in_=sc_[:, hh, :], func=AF.Exp, bias=st_[:, 1, hh:hh + 1], accum_out=st_[:, 2, hh:hh + 1]),
                     r=[f"sc{s}", f"st{s}"], w=[f"pp{s}", f"st{s}"])
            k.op("act", lambda e, st_=st_: e.activation(out=st_[:, 3, :], in_=st_[:, 3, :], func=AF.Exp), r=[f"st{s}"], w=[f"st{s}"])
            k.op("dve", lambda e, st_=st_: e.tensor_tensor(out=st_[:, 4, :], in0=st_[:, 2, :], in1=st_[:, 3, :], op=ALU.add), r=[f"st{s}"], w=[f"st{s}"])
            k.op("dve", lambda e, st_=st_: e.reciprocal(out=st_[:, 4, :], in_=st_[:, 4, :]), r=[f"st{s}"], w=[f"st{s}"])
            for hh in range(2):
                for kb in range(2):
                    k.op("pe", lambda e, pp_=pp_, hh=hh, kb=kb: e.transpose(ppT[:, hh * 2 + kb, :], pp_[:, hh, kb * 128:(kb + 1) * 128], idb[:]), r=[f"pp{s}", "idb"], w=["ppT"], inc=(hh == 1 and kb == 1))
            k.op("act", lambda e, pT_=pT_: e.copy(out=pT_[:], in_=ppT[:]), r=["ppT"], w=[f"pT{s}"])
            for hh in range(2):
                for kb in range(2):
                    k.op("pe", lambda e, pT_=pT_, hh=hh, kb=kb, n=n, kvh=kvh: e.matmul(po[:, hh, :], pT_[:, hh * 2 + kb, :], vtok[:, n + kb, kvh * 64:(kvh + 1) * 64], start=(kb == 0), stop=(kb == 1)),
                         r=[f"pT{s}", "vtok", "vtokh"], w=["po"], inc=(hh == 1 and kb == 1))
            k.op("dve", lambda e, on_=on_, st_=st_: e.tensor_tensor(out=on_[:], in0=po[:], in1=st_[:, 4, :].unsqueeze(2).broadcast_to([128, 2, 64]), op=ALU.mult), r=["po", f"st{s}"], w=[f"on{s}"])
            k.op("pe", lambda e, on_=on_: e.transpose(poT[:], on_[:].rearrange("p a b -> p (a b)"), idb[:]), r=[f"on{s}", "idb"], w=["poT"])
            k.op("dve", lambda e, n=n, ost=ost: e.tensor_tensor(out=obT[ost][:, n * 128:(n + 1) * 128], in0=poT[:], in1=gate[:, n * 128:(n + 1) * 128], op=ALU.mult), r=["poT", "gate"], w=[f"obT{ost}"])
        k.dma("sp", o_b[qc, :, :], obT[ost][:], f"obT{ost}", r=[f"obT{ost}"])
    return k.finish()


def build_c(T, full=True):
    k = K(); c = Ctx(k, T, nslab=4, npsum=8, slabs=None if full else SL_C)
    TG = min(512, T); NG = T // TG
    hh_d = k.din("h_halo", [8, 128, 2], F32)
    cw_d = k.din("conv_w", [128, 8, 3], F32)
    o_c = k.dout("o_c", [8, 128, T], BF16)
    cw = k.sb("cw", [128, 8, 3], F32)
    k.dma("sp", cw[:], cw_d[:, :, :], "cw", w=["cw"])
    hb = k.sb("hb", [128, 2 + T], F32); cc_ = k.sb("ccs", [128, TG], F32); y = k.sb("y", [128, TG], F32); gs = k.sb("gs", [128, TG], F32)
    ocT = [k.sb(f"ocT{i}", [128, T], BF16) for i in range(2)]
    for j in range(8):
        k.dma("sp", hb[:, 0:2], hh_d[j, :, :], "hbh", w=["hbh"])
        wb, wbt = c.slab(C_CB + j); wc, wct = c.slab(C_CC + j); wx, wxt = c.slab(C_CX + j); wg, wgt = c.slab(C_CG + j)
        os_ = j % 2
        for tg in range(NG):
            t0 = tg * TG
            Pc, pct = c.fm(wc, wct, t0, TG); Px, pxt = c.fm(wx, wxt, t0, TG); Pb, pbt = c.fm(wb, wbt, t0, TG); Pg, pgt = c.fm(wg, wgt, t0, TG)
            k.op("act", lambda e, Pc=Pc: e.copy(out=cc_[:], in_=Pc[:, 0:TG]), r=[pct], w=["ccs"])
            k.op("dve", lambda e, Px=Px, t0=t0: e.tensor_tensor(out=hb[:, 2 + t0:2 + t0 + TG], in0=cc_[:], in1=Px[:, 0:TG], op=ALU.mult), r=["ccs", pxt], w=["hb"])
            k.op("dve", lambda e, j=j, t0=t0: e.tensor_scalar(out=y[:], in0=hb[:, t0:t0 + TG], scalar1=cw[:, j, 0:1], scalar2=None, op0=ALU.mult), r=["hb", "hbh", "cw"], w=["y"])
            k.op("dve", lambda e, j=j, t0=t0: e.scalar_tensor_tensor(out=y[:], in0=hb[:, t0 + 1:t0 + 1 + TG], scalar=cw[:, j, 1:2], in1=y[:], op0=ALU.mult, op1=ALU.add), r=["hb", "hbh", "cw", "y"], w=["y"])
            k.op("dve", lambda e, j=j, t0=t0: e.scalar_tensor_tensor(out=y[:], in0=hb[:, t0 + 2:t0 + 2 + TG], scalar=cw[:, j, 2:3], in1=y[:], op0=ALU.mult, op1=ALU.add), r=["hb", "cw", "y"], w=["y"])
            k.op("act", lambda e, Pg=Pg: e.activation(out=gs[:], in_=Pg[:, 0:TG], func=AF.Silu), r=[pgt], w=["gs"])
            k.op("dve", lambda e, Pb=Pb: e.tensor_tensor(out=y[:], in0=y[:], in1=Pb[:, 0:TG], op=ALU.mult), r=["y", pbt], w=["y"])
            k.op("pool", lambda e, t0=t0, os_=os_: e.tensor_tensor(out=ocT[os_][:, t0:t0 + TG], in0=y[:], in1=gs[:], op=ALU.mult), r=["y", "gs"], w=[f"ocT{os_}"])
        k.dma("sp", o_c[j, :, :], ocT[os_][:], f"ocT{os_}", r=[f"ocT{os_}"])
    return k.finish()


def build_d(T, full=True):
    k = K(); c = Ctx(k, T, nslab=3, npsum=7, load_xT=False, slabs=None if full else SL_D)
    TG = min(512, T); NG = T // TG; NBG = TG // 128
    ident = k.din("ident", [128, 128], BF16)
    x_d = k.din("x", [T, 2048], F32)
    o_d = [k.din(n, [8, 128, T], BF16) for n in ("o_a", "o_b", "o_c")]
    wp_d = [k.din(n, [16, 128, 8 * 128], F32) for n in ("wpa_t", "wpb_t", "wpc_t")]
    wo_d = k.din("wout_t", [8, 128, 16 * 256], F32)
    lng_d = k.din("ln_g", [128, 2048], F32); lnb_d = k.din("ln_b", [128, 2048], F32)
    xn_d = k.dout("x_new", [T, 2048], F32)
    xTn_d = k.dout("xT_new", [16, 128, T], BF16)
    idb = k.sb("idb", [128, 128], BF16)
    k.dma("sp", idb[:], ident[:, :], "id", w=["idb"])
    lng = k.sb("lng", [128, 2048], F32); lnb = k.sb("lnb", [128, 2048], F32)
    k.dma("sp", lng[:], lng_d[:, :], "lng", w=["lng"]); k.dma("sp", lnb[:], lnb_d[:, :], "lnb", w=["lnb"])
    xg = k.sb("xg", [128, 16, TG], BF16)
    ob = [k.sb(f"ob{i}", [128, 8, TG], BF16) for i in range(3)]
    mg = k.sb("mg", [128, 16, TG], BF16)
    wps = [[k.sb(f"wp{i}_{j}", [128, 8, 128], BF16) for j in range(2)] for i in range(3)]
    wos = [k.sb(f"wo{j}", [128, 16, 256], BF16) for j in range(2)]
    z = k.sb("z", [128, 2, 2048], F32)
    sa = k.sb("sa", [128, TG], F32); acc = k.sb("acc", [128, TG], F32); tt = k.sb("tt", [128, TG], F32)
    xbf = k.sb("xbf", [128, 2048], BF16)
    xTn = [k.sb(f"xTn{i}", [128, 16, 128], BF16) for i in range(2)]
    stats = k.sb("stats", [128, 4, 6], F32); mv = k.sb("mv", [128, 4], F32)
    ptr = k.ps("ptr", [128, 4, 128], BF16)
    nwo = 0; nblk = 0
    for tg in range(NG):
        t0 = tg * TG
        for kc in range(16):
            k.dma("sp", xg[:, kc, :], c.xT_d[kc, :, t0:t0 + TG], "xg", w=["xg"])
        for i in range(3):
            k.dma("sp", ob[i][:], o_d[i][:, :, t0:t0 + TG].rearrange("a p t -> p a t"), f"ob{i}", w=[f"ob{i}"])
        for fc in range(16):
            ws = fc % 2
            for i in range(3):
                k.dma("pool", wps[i][ws][:].rearrange("p a b -> p (a b)"), wp_d[i][fc, :, :], f"wp{i}_{ws}", w=[f"wp{i}_{ws}"])
            for i, cm in enumerate((C_MA, C_MB, C_MC)):
                wm, wmt = c.slab(cm + fc)
                Pm, pmt = c.bank()
                for kc in range(16):
                    k.op("pe", lambda e, Pm=Pm, wm=wm, kc=kc: e.matmul(Pm[:, 0:TG], wm[:, kc, :], xg[:, kc, :], start=(kc == 0), stop=(kc == 15)), r=[wmt, "xg"], w=[pmt], inc=(kc == 15))
                Py, pyt = c.bank()
                for kc in range(8):
                    k.op("pe", lambda e, Py=Py, i=i, ws=ws, kc=kc: e.matmul(Py[:, 0:TG], wps[i][ws][:, kc, :], ob[i][:, kc, :], start=(kc == 0), stop=(kc == 7)), r=[f"wp{i}_{ws}", f"ob{i}"], w=[pyt], inc=(kc == 7))
                k.op("act", lambda e, Pm=Pm: e.activation(out=sa[:], in_=Pm[:, 0:TG], func=AF.Sigmoid), r=[pmt], w=["sa"])
                if i == 0:
                    k.op("dve", lambda e, Py=Py: e.tensor_tensor(out=acc[:], in0=sa[:], in1=Py[:, 0:TG], op=ALU.mult), r=["sa", pyt], w=["acc"])
                else:
                    k.op("dve", lambda e, Py=Py: e.tensor_tensor(out=tt[:], in0=sa[:], in1=Py[:, 0:TG], op=ALU.mult), r=["sa", pyt], w=["tt"])
                    if i == 1:
                        k.op("pool", lambda e: e.tensor_tensor(out=acc[:], in0=acc[:], in1=tt[:], op=ALU.add), r=["acc", "tt"], w=["acc"])
                    else:
                        k.op("pool", lambda e, fc=fc: e.tensor_tensor(out=mg[:, fc, :], in0=acc[:], in1=tt[:], op=ALU.add), r=["acc", "tt"], w=["mg"])
        for half in range(NBG // 2):
            tb0 = t0 + half * 256
            k.dma("sp", z[:], x_d[tb0:tb0 + 256, :].rearrange("(b p) f -> p b f", p=128), "z", w=["z"])
            for n in range(8):
                wsl_ = nwo % 2; nwo += 1
                k.dma("pool", wos[wsl_][:].rearrange("p a b -> p (a b)"), wo_d[n, :, :], f"wo{wsl_}", w=[f"wo{wsl_}"])
                for b in range(2):
                    Po, pot = c.bank()
                    tl = half * 256 + b * 128
                    for kc in range(16):
                        k.op("pe", lambda e, Po=Po, kc=kc, tl=tl, wsl_=wsl_: e.matmul(Po[:, 0:256], mg[:, kc, tl:tl + 128], wos[wsl_][:, kc, :], start=(kc == 0), stop=(kc == 15)), r=["mg", f"wo{wsl_}"], w=[pot], inc=(kc == 15))
                    k.op("dve", lambda e, Po=Po, b=b, n=n: e.scalar_tensor_tensor(out=z[:, b, n * 256:(n + 1) * 256], in0=z[:, b, n * 256:(n + 1) * 256], scalar=ALPHA, in1=Po[:, 0:256], op0=ALU.mult, op1=ALU.add), r=["z", pot], w=["z"])
            for b in range(2):
                for q in range(4):
                    k.op("dve", lambda e, b=b, q=q: e.bn_stats(out=stats[:, q, :], in_=z[:, b, q * 512:(q + 1) * 512]), r=["z"], w=["stats"])
                k.op("dve", lambda e: e.bn_aggr(out=mv[:, 0:2], in_=stats[:].rearrange("p a b -> p (a b)")), r=["stats"], w=["mv"])
                k.op("act", lambda e: e.activation(out=mv[:, 2:3], in_=mv[:, 1:2], func=AF.Sqrt, bias=LN_EPS), r=["mv"], w=["mv"])
                k.op("dve", lambda e: e.reciprocal(out=mv[:, 2:3], in_=mv[:, 2:3]), r=["mv"], w=["mv"])
                k.op("dve", lambda e: e.scalar_tensor_tensor(out=mv[:, 3:4], in0=mv[:, 0:1], scalar=-1.0, in1=mv[:, 2:3], op0=ALU.mult, op1=ALU.mult), r=["mv"], w=["mv"])
                k.op("act", lambda e, b=b: e.activation(out=z[:, b, :], in_=z[:, b, :], func=AF.Identity, scale=mv[:, 2:3], bias=mv[:, 3:4]), r=["z", "mv"], w=["z"])
                k.op("dve", lambda e, b=b: e.tensor_tensor(out=z[:, b, :], in0=z[:, b, :], in1=lng[:], op=ALU.mult), r=["z", "lng"], w=["z"])
                k.op("pool", lambda e, b=b: e.tensor_tensor(out=z[:, b, :], in0=z[:, b, :], in1=lnb[:], op=ALU.add), r=["z", "lnb"], w=["z"])
                tb = tb0 + b * 128
                k.dma("sp", xn_d[tb:tb + 128, :], z[:, b, :], "zout", r=["z"])
                k.op("act", lambda e, b=b: e.copy(out=xbf[:], in_=z[:, b, :]), r=["z"], w=["xbf"])
                xs_ = nblk % 2; nblk += 1
                for g4 in range(4):
                    for j in range(4):
                        kc = g4 * 4 + j
                        k.op("pe", lambda e, j=j, kc=kc: e.transpose(ptr[:, j, :], xbf[:, kc * 128:(kc + 1) * 128], idb[:]), r=["xbf", "idb"], w=["ptr"], inc=(j == 3))
                    k.op("dve", lambda e, g4=g4, xs_=xs_: e.tensor_copy(out=xTn[xs_][:, g4 * 4:g4 * 4 + 4, :], in_=ptr[:]), r=["ptr"], w=[f"xTn{xs_}"])
                k.dma("sp", xTn_d[:, :, tb:tb + 128].rearrange("k p t -> p k t"), xTn[xs_][:], f"xTn{xs_}", r=[f"xTn{xs_}"])
    return k.finish()


def w_in_t(w):
    return np.ascontiguousarray(w.reshape(16, 128, 132, 128).transpose(2, 1, 0, 3)).reshape(132, 128, 2048)

def slab_t(w, ncol):
    K, C = w.shape
    return np.ascontiguousarray(w.reshape(K // 128, 128, C // ncol, ncol).transpose(2, 1, 0, 3)).reshape(C // ncol, 128, (K // 128) * ncol)

def wk_dup_t(w):
    out = []
    for kv in range(4):
        cols = w[:, 5120 + kv * 64: 5120 + (kv + 1) * 64]
        out.append(np.concatenate([cols, cols], axis=1))
    return slab_t(np.concatenate(out, axis=1), 128)

def lb_lay(lb_param, l):
    a = np.ascontiguousarray(lb_param.T.reshape(8, 128, 4).transpose(1, 0, 2))
    sel = np.zeros((128, 8, 4), np.float32)
    for j in range(4):
        if 1 <= j <= l:
            sel[:, :, j] = 1.0
    return sel, a.astype(np.float32)


def t5_bucket_tab():
    import math
    d = np.arange(128)
    logd = np.log(np.maximum(d, 1).astype(np.float32) / 16) / math.log(128 / 16)
    large = np.minimum(16 + (logd * 16).astype(np.int32), 31)
    return np.where(d < 16, d, large)

def band_bias(rel_bias):
    i = np.arange(128)[:, None]; j = np.arange(256)[None, :]
    rel = np.clip(128 + i - j, 0, 127)
    bucket = t5_bucket_tab()[rel]
    return np.ascontiguousarray(np.transpose(rel_bias[bucket], (2, 0, 1))).astype(np.float32)

def neg_mask():
    i = np.arange(128)[:, None]; j = np.arange(256)[None, :]
    rel = 128 + i - j
    return np.where((rel >= 0) & (rel < 128), 0.0, -30000.0).astype(np.float32)


def build_prep(T):
    k = K()
    NB = T // 128
    x = k.din("x", [T, 2048], F32)
    ident = k.din("ident", [128, 128], BF16)
    xT = k.dout("xT", [16, 128, T], BF16)
    idb = k.sb("idb", [128, 128], BF16)
    xb = [k.sb(f"xb{i}", [128, 2048], BF16) for i in range(2)]
    xTs = k.sb("xTs", [128, 16, T], BF16)
    pt = [k.ps(f"pt{i}", [128, 4, 128], BF16) for i in range(2)]
    k.dma("sp", idb[:], ident[:, :], "id", w=["idb"])
    n = 0
    for tb in range(NB):
        s = tb % 2
        k.dma("pool", xb[s][:], x[tb * 128:(tb + 1) * 128, :], f"xb{s}", w=[f"xb{s}"])
        for g in range(4):
            p = n % 2; n += 1
            for j in range(4):
                kc = g * 4 + j
                k.op("pe", lambda e, p=p, j=j, s=s, kc=kc: e.transpose(pt[p][:, j, :], xb[s][:, kc * 128:(kc + 1) * 128], idb[:]),
                     r=[f"xb{s}", "idb"], w=[f"pt{p}"], inc=(j == 3))
            eng = "act" if g % 2 == 0 else "dve"
            if eng == "act":
                k.op("act", lambda e, p=p, g=g, tb=tb: e.copy(out=xTs[:, g * 4:(g + 1) * 4, tb * 128:(tb + 1) * 128], in_=pt[p][:]),
                     r=[f"pt{p}"], w=[("xTs", tb)])
            else:
                k.op("dve", lambda e, p=p, g=g, tb=tb: e.tensor_copy(out=xTs[:, g * 4:(g + 1) * 4, tb * 128:(tb + 1) * 128], in_=pt[p][:]),
                     r=[f"pt{p}"], w=[("xTs", tb)])
    k.dma("sp", xT.rearrange("k p t -> p k t"), xTs[:], "out", r=[("xTs", tb) for tb in range(NB)])
    return k.finish()


T_CORE = 2048
NCORES = 8
_PROGS = {}


def _prog(name):
    if name not in _PROGS:
        b = {"prep": lambda: build_prep(T_CORE), "p1": lambda: build_p1(T_CORE, False), "a": lambda: build_a(T_CORE, False),
             "b": lambda: build_b(T_CORE, False), "c": lambda: build_c(T_CORE, False), "d": lambda: build_d(T_CORE, False)}[name]
        _PROGS[name] = b()
    return _PROGS[name]


def _launch(name, in_maps):
    res = run_bass_kernel_spmd(_prog(name), in_maps, core_ids=list(range(NCORES)))
    return res.results


def kernel(x, w_in, w_proj_hgrn, w_proj_attn, w_proj_conv, w_out, lb_param, hgrn_norm_g, attn_sinks, conv_w, rel_bias, ln_g, ln_b):
    f32 = np.float32
    x = np.asarray(x, f32)
    B, S, Dm = x.shape
    T = T_CORE; R = S // T
    assert B * R == NCORES
    ident = np.eye(128, dtype=f32).astype(NPBF)
    tri = (np.arange(64)[:, None] <= np.arange(64)[None, :]).astype(np.int32)
    negmask = neg_mask()
    bias = band_bias(np.asarray(rel_bias, f32))
    segs = [np.ascontiguousarray(x[c // R, (c % R) * T:(c % R + 1) * T, :]) for c in range(NCORES)]
    res = _launch("prep", [{"x": segs[c], "ident": ident} for c in range(NCORES)])
    xT = [np.asarray(res[c]["xT"]) for c in range(NCORES)]
    xcur = segs
    z_s = np.zeros((8, 128, 128), f32); z_d = np.zeros((128, 8), f32)
    z_k = np.zeros((4, 128, 128), NPBF); z_v = np.zeros((128, 256), NPBF); z_h = np.zeros((8, 128, 2), f32)
    for l in range(4):
        wl = np.asarray(w_in[l], f32)
        wt = w_in_t(wl)
        num, den = lb_lay(np.asarray(lb_param, f32), l)
        w_p1 = np.ascontiguousarray(wt[SL_P1]); wkd = wk_dup_t(wl)
        r1 = _launch("p1", [{"xT": xT[c], "w_in_t": w_p1, "wk_dup_t": wkd, "lb_num": num, "lb_den": den, "ident": ident} for c in range(NCORES)])
        pred = lambda c, j: (c - j) if (c % R) - j >= 0 else None
        w_a = np.ascontiguousarray(wt[SL_A])
        ng = np.ascontiguousarray(np.asarray(hgrn_norm_g[l], f32).reshape(8, 128).T)
        ins = []
        for c in range(NCORES):
            ps = [pred(c, 3), pred(c, 2), pred(c, 1)]
            sp = np.stack([np.asarray(r1[p]["s_loc"]) if p is not None else z_s for p in ps])
            dp = np.stack([np.asarray(r1[p]["d_seg"]) if p is not None else z_d for p in ps])
            ins.append({"xT": xT[c], "w_in_t": w_a, "lb_num": num, "lb_den": den, "ident": ident, "tri": tri, "s_prev": sp, "d_prev": dp, "norm_g": ng})
        ra = _launch("a", ins)
        w_b = np.ascontiguousarray(wt[SL_B])
        sinks = np.ascontiguousarray(np.broadcast_to(np.asarray(attn_sinks[l], f32)[None, :], (128, 16)))
        ins = []
        for c in range(NCORES):
            p = pred(c, 1)
            ins.append({"xT": xT[c], "w_in_t": w_b, "ident": ident,
                        "kT_halo": np.asarray(r1[p]["kT_halo"]) if p is not None else z_k,
                        "v_halo": np.asarray(r1[p]["v_halo"]) if p is not None else z_v,
                        "bias": bias, "negmask": negmask, "firstneg": np.full((128, 1), NEG if p is None else 0.0, f32), "sinks": sinks})
        rb = _launch("b", ins)
        w_c = np.ascontiguousarray(wt[SL_C])
        cw = np.ascontiguousarray(np.asarray(conv_w[l], f32).T.reshape(8, 128, 3).transpose(1, 0, 2))
        ins = []
        for c in range(NCORES):
            p = pred(c, 1)
            ins.append({"xT": xT[c], "w_in_t": w_c, "h_halo": np.asarray(r1[p]["h_halo"]) if p is not None else z_h, "conv_w": cw})
        rc = _launch("c", ins)
        w_d = np.ascontiguousarray(wt[SL_D])
        wpa = slab_t(np.asarray(w_proj_hgrn[l], f32), 128); wpb = slab_t(np.asarray(w_proj_attn[l], f32), 128); wpc = slab_t(np.asarray(w_proj_conv[l], f32), 128)
        wo = slab_t(np.asarray(w_out[l], f32), 256)
        g_b = np.ascontiguousarray(np.broadcast_to(np.asarray(ln_g[l], f32)[None], (128, 2048)))
        b_b = np.ascontiguousarray(np.broadcast_to(np.asarray(ln_b[l], f32)[None], (128, 2048)))
        ins = []
        for c in range(NCORES):
            ins.append({"xT": xT[c], "x": xcur[c], "w_in_t": w_d, "ident": ident, "o_a": np.asarray(ra[c]["o_a"]), "o_b": np.asarray(rb[c]["o_b"]), "o_c": np.asarray(rc[c]["o_c"]),
                        "wpa_t": wpa, "wpb_t": wpb, "wpc_t": wpc, "wout_t": wo, "ln_g": g_b, "ln_b": b_b})
        rd = _launch("d", ins)
        xcur = [np.asarray(rd[c]["x_new"]) for c in range(NCORES)]
        xT = [np.asarray(rd[c]["xT_new"]) for c in range(NCORES)]
    out = np.empty((B, S, Dm), f32)
    for c in range(NCORES):
        out[c // R, (c % R) * T:(c % R + 1) * T, :] = xcur[c]
    return out
```

```python
import numpy as np, ml_dtypes, time
from contextlib import ExitStack
import concourse.bass as bass
import concourse.mybir as mybir
from concourse.bass_utils import run_bass_kernel_spmd

F32 = mybir.dt.float32; BF16 = mybir.dt.bfloat16; I32 = mybir.dt.int32; U8 = mybir.dt.uint8
AF = mybir.ActivationFunctionType; ALU = mybir.AluOpType; AX = mybir.AxisListType
NPBF = ml_dtypes.bfloat16
ENGS = ("pe", "act", "dve", "pool", "sp")


class K:
    def __init__(self, same_sync=("act", "dve", "pool"), arena_cols=0):
        self.nc = bass.Bass("TRN2", target_bir_lowering=False)
        self.es = ExitStack()
        self.io = None
        self.arena = None
        if arena_cols:
            self.arena = self.es.enter_context(self.nc.sbuf_tensor("sb_arena", [128, arena_cols], F32))
            self.arena_cols = arena_cols
            self.banks = [self.es.enter_context(self.nc.psum_tensor(f"ps_bank{i}", [128, 512], F32)) for i in range(8)]
            self.a_off = 0; self.b_off = 0
        self.ncc = 0
        self.q = {e: [] for e in ENGS}
        self.cnt = {e: 0 for e in ENGS}
        self.pend = {e: False for e in ENGS}
        self.seen = {e: {} for e in ENGS}
        self.lastw = {}
        self.rd = {}
        self.dcnt = {}
        self.same_sync = set(same_sync)
        self.n_inst = 0

    def din(self, name, shape, dt):
        if self.io is not None:
            ap = self.io[name]
            assert list(ap.shape) == list(shape) and ap.dtype == dt, (name, ap.shape, shape, ap.dtype, dt)
            return ap
        return self.nc.dram_tensor(name, list(shape), dt, kind="ExternalInput").ap()

    def dout(self, name, shape, dt):
        if self.io is not None:
            ap = self.io[name]
            assert list(ap.shape) == list(shape) and ap.dtype == dt, (name, ap.shape, shape, ap.dtype, dt)
            return ap
        return self.nc.dram_tensor(name, list(shape), dt, kind="ExternalOutput").ap()

    def xin(self, name, shape, dt):
        return self.nc.dram_tensor(name, list(shape), dt, kind="ExternalInput").ap()

    def xout(self, name, shape, dt):
        return self.nc.dram_tensor(name, list(shape), dt, kind="ExternalOutput").ap()

    @staticmethod
    def _view(base, shape, dt):
        isz = 4 if dt in (F32, I32) else 2
        n = 1
        for d in shape[1:]:
            n *= d
        cols = (n * isz + 3) // 4
        ap = base[0:shape[0], 0:cols]
        if dt != F32:
            ap = ap.bitcast(dt)
        if len(shape) == 3:
            ap = ap.rearrange("p (a b) -> p a b", b=shape[2])
        elif len(shape) == 4:
            ap = ap.rearrange("p (a b c) -> p a b c", b=shape[2], c=shape[3])
        return ap, cols

    def stage_begin(self):
        self.a_off = 0; self.b_off = 0

    def barrier(self):
        for e in ENGS:
            assert not self.pend[e]
        for E in ENGS:
            waits = []
            for F in ENGS:
                if F != E and self.cnt[F] > self.seen[E].get(("eng", F), 0):
                    self.seen[E][("eng", F)] = self.cnt[F]; waits.append((("eng", F), self.cnt[F]))
            for sname, v in self.dcnt.items():
                if v > self.seen[E].get(("dma", sname), 0):
                    self.seen[E][("dma", sname)] = v; waits.append((("dma", sname), v))
            self.q[E].append(("wait", waits))
        self.lastw = {}; self.rd = {}

    def cc_allreduce(self, in_t, out_t, ngroup, r=(), w=()):
        waits = self._deps("pool", r, w)
        self.dcnt["cc"] = self.dcnt.get("cc", 0) + 1
        self.q["pool"].append(("cc", waits, (in_t, out_t, ngroup)))
        self._commit(r, w, (("dma", "cc"), self.dcnt["cc"]))
        self.n_inst += 1

    def dscr(self, name, shape, dt):
        return self.nc.dram_tensor(name, list(shape), dt).ap()

    def sb(self, name, shape, dt):
        if self.arena is None:
            return self.es.enter_context(self.nc.sbuf_tensor("sb_" + name, list(shape), dt))
        ap, cols = self._view(self.arena[:, self.a_off:], list(shape), dt)
        self.a_off += (cols + 15) // 16 * 16
        assert self.a_off <= self.arena_cols, f"SBUF arena overflow at {name}: {self.a_off * 4} bytes"
        return ap

    def ps(self, name, shape, dt):
        if self.arena is None:
            return self.es.enter_context(self.nc.psum_tensor("ps_" + name, list(shape), dt))
        assert self.b_off < 8, f"PSUM overflow at {name}"
        ap, cols = self._view(self.banks[self.b_off][:, :], list(shape), dt)
        self.b_off += 1
        return ap

    def _deps(self, E, r, w):
        deps = set()
        for t in r:
            if t in self.lastw:
                deps.add(self.lastw[t])
        for t in w:
            if t in self.lastw:
                deps.add(self.lastw[t])
            for x in self.rd.get(t, ()):
                deps.add(x)
        waits = []
        for key, val in sorted(deps, key=lambda d: (str(d[0]), d[1])):
            if key == ("eng", E) and E not in self.same_sync:
                continue
            if self.seen[E].get(key, 0) >= val:
                continue
            self.seen[E][key] = val
            waits.append((key, val))
        return waits

    def _commit(self, r, w, stamp):
        for t in w:
            self.lastw[t] = stamp
            self.rd[t] = []
        for t in r:
            self.rd.setdefault(t, []).append(stamp)

    def op(self, E, fn, r=(), w=(), inc=True):
        waits = self._deps(E, r, w)
        if inc:
            self.cnt[E] += 1
            n = self.cnt[E]
        else:
            n = self.cnt[E] + 1
        self.pend[E] = not inc
        self.q[E].append(("op", waits, fn, inc))
        self._commit(r, w, (("eng", E), n))
        self.n_inst += 1

    def dma(self, E, out, in_, sem, r=(), w=()):
        waits = self._deps(E, r, w)
        self.dcnt[sem] = self.dcnt.get(sem, 0) + 16
        self.q[E].append(("dma", waits, (out, in_), sem))
        self._commit(r, w, (("dma", sem), self.dcnt[sem]))
        self.n_inst += 1

    def finish(self):
        nc = self.nc
        for e in ENGS:
            assert not self.pend[e], f"engine {e} ends with a non-incrementing op"
        sems = {}
        for e in ENGS:
            sems[("eng", e)] = self.es.enter_context(nc.semaphore("s_" + e))
        for s in self.dcnt:
            sems[("dma", s)] = self.es.enter_context(nc.semaphore("d_" + s))
        block = self.es.enter_context(nc.Block())
        final = [(("dma", s), v) for s, v in self.dcnt.items()]

        def replay(E, eng):
            for item in self.q[E]:
                kind, waits = item[0], item[1]
                for key, val in waits:
                    eng.wait_ge(sems[key], val)
                if kind == "wait":
                    continue
                if kind == "cc":
                    in_t, out_t, ngroup = item[2]
                    eng.collective_compute("AllReduce", mybir.AluOpType.add, replica_groups=[list(range(ngroup))],
                                           ins=[in_t.ap().opt()], outs=[out_t.ap().opt()]).then_inc(sems[("dma", "cc")])
                    continue
                if kind == "op":
                    ins = item[2](eng)
                    if item[3]:
                        ins.then_inc(sems[("eng", E)], 1)
                else:
                    out, in_ = item[2]
                    eng.dma_start(out=out, in_=in_).then_inc(sems[("dma", item[3])], 16)
            if E == "sp":
                for key, val in final:
                    eng.wait_ge(sems[key], val)

        @block.tensor
        def _(eng):
            replay("pe", eng)

        @block.scalar
        def _(eng):
            replay("act", eng)

        @block.vector
        def _(eng):
            replay("dve", eng)

        @block.gpsimd
        def _(eng):
            replay("pool", eng)

        @block.sync
        def _(eng):
            replay("sp", eng)

        self.es.close()
        return nc


D = 2048; KC = 16
C_AQ, C_AF, C_AI, C_AG = 0, 8, 16, 24
C_BQ, C_BK, C_BV, C_BG = 32, 40, 42, 44
C_CB, C_CC, C_CX, C_CG = 52, 60, 68, 76
C_MA, C_MB, C_MC = 84, 100, 116
RMS_EPS = 1e-6; LN_EPS = 1e-5
ALPHA = (2.0 * 4) ** 0.25
NEG = -30000.0


class Ctx:
    def __init__(self, k, T, nslab=3, npsum=8, load_xT=True, slabs=None):
        self.k = k; self.T = T
        self.slabs = list(range(132)) if slabs is None else list(slabs)
        self.xT_d = k.din("xT", [16, 128, T], BF16)
        self.win = k.din("w_in_t", [len(self.slabs), 128, 16 * 128], F32)
        self.xTs = k.sb("xTs", [128, 16, T if load_xT else 8], BF16)
        self.wsl = [k.sb(f"wsl{i}", [128, 16, 128], BF16) for i in range(nslab)]
        self.nsl = 0
        self.pb = [k.ps(f"pb{i}", [128, 512], F32) for i in range(npsum)]
        self.npb = 0
        for kc in range(16 if load_xT else 0):
            k.dma("sp", self.xTs[:, kc, :], self.xT_d[kc, :, :], "xT", w=[("xT", kc)])
        for kc in range(16 if load_xT else 0):
            k.lastw[("xT", kc)] = (("dma", "xT"), k.dcnt["xT"])
        self.xT_tok = [("xT", kc) for kc in range(16)]

    def slab(self, j, src=None):
        k = self.k
        s = self.nsl % len(self.wsl); self.nsl += 1
        tok = f"wsl{s}"
        srcap = self.win[self.slabs.index(j), :, :] if src is None else src[j, :, :]
        k.dma("pool", self.wsl[s][:].rearrange("p a b -> p (a b)"), srcap, tok, w=[tok])
        return self.wsl[s], tok

    def bank(self):
        b = self.npb % len(self.pb); self.npb += 1
        return self.pb[b], f"pb{b}"

    def fm(self, wt, wtok, t0, n, c0=0, c1=128):
        k = self.k
        P, ptok = self.bank()
        for kc in range(16):
            k.op("pe", lambda e, P=P, wt=wt, kc=kc, t0=t0, n=n: e.matmul(P[0:c1 - c0, 0:n], wt[:, kc, c0:c1], self.xTs[:, kc, t0:t0 + n], start=(kc == 0), stop=(kc == 15)),
                 r=[wtok, ("xT", kc)], w=[ptok], inc=(kc == 15))
        return P, ptok

    def tm(self, wt, wtok, t0, m, P, ptok, c0, first=True):
        k = self.k
        for kc in range(16):
            k.op("pe", lambda e, P=P, wt=wt, kc=kc, t0=t0, m=m, c0=c0: e.matmul(P[0:m, c0:c0 + 128], self.xTs[:, kc, t0:t0 + m], wt[:, kc, :], start=(kc == 0), stop=(kc == 15)),
                 r=[wtok, ("xT", kc)], w=[ptok], inc=(kc == 15))


def lower_bound(k, c):
    lbn_d = k.din("lb_num", [128, 8, 4], F32); lbd_d = k.din("lb_den", [128, 8, 4], F32)
    lbn = k.sb("lbn", [128, 8, 4], F32); lbd = k.sb("lbd", [128, 8, 4], F32)
    lbs = k.sb("lbs", [128, 4, 8], F32)
    k.dma("sp", lbn[:], lbn_d[:, :, :], "lbn", w=["lbn"])
    k.dma("sp", lbd[:], lbd_d[:, :, :], "lbd", w=["lbd"])
    k.op("act", lambda e: e.activation(out=lbd[:], in_=lbd[:], func=AF.Exp), r=["lbd"], w=["lbd"])
    k.op("dve", lambda e: e.tensor_tensor(out=lbn[:], in0=lbn[:], in1=lbd[:], op=ALU.mult), r=["lbn", "lbd"], w=["lbn"])
    k.op("dve", lambda e: e.tensor_reduce(out=lbs[:, 0, :], in_=lbn[:], axis=AX.X, op=ALU.add), r=["lbn"], w=["lbs"])
    k.op("dve", lambda e: e.tensor_reduce(out=lbs[:, 1, :], in_=lbd[:], axis=AX.X, op=ALU.add), r=["lbd"], w=["lbs"])
    k.op("dve", lambda e: e.reciprocal(out=lbs[:, 1, :], in_=lbs[:, 1, :]), r=["lbs"], w=["lbs"])
    k.op("dve", lambda e: e.tensor_tensor(out=lbs[:, 2, :], in0=lbs[:, 0, :], in1=lbs[:, 1, :], op=ALU.mult), r=["lbs"], w=["lbs"])
    k.op("dve", lambda e: e.tensor_scalar(out=lbs[:, 3, :], in0=lbs[:, 2, :], scalar1=-1.0, scalar2=1.0, op0=ALU.mult, op1=ALU.add), r=["lbs"], w=["lbs"])
    return lbs


def forget_gate(k, c, h, lbs, fbuf, ftok):
    T = c.T
    wt, wtok = c.slab(C_AF + h)
    for tg in range(T // 512 if T >= 512 else 1):
        n = min(512, T)
        P, ptok = c.fm(wt, wtok, tg * 512, n)
        k.op("act", lambda e, P=P, tg=tg, n=n: e.activation(out=fbuf[:, tg * 512:tg * 512 + n], in_=P[:, 0:n], func=AF.Sigmoid), r=[ptok], w=[ftok])
    k.op("dve", lambda e: e.tensor_scalar(out=fbuf[:], in0=fbuf[:], scalar1=lbs[:, 3, h:h + 1], scalar2=lbs[:, 2, h:h + 1], op0=ALU.mult, op1=ALU.add), r=[ftok, "lbs"], w=[ftok])


SL_P1 = list(range(8, 24)) + [42, 43] + list(range(60, 76))
SL_A = list(range(0, 32))
SL_B = list(range(32, 52))
SL_C = list(range(52, 84))
SL_D = list(range(84, 132))


def build_p1(T, full=True, k=None, halo_dt=BF16):
    own = k is None
    k = K() if own else k
    c = Ctx(k, T, npsum=7, slabs=None if full else SL_P1)
    NB = T // 128
    ident = k.din("ident", [128, 128], BF16)
    wkdup = k.din("wk_dup_t", [4, 128, 16 * 128], F32)
    o_sloc = k.dout("s_loc", [8, 128, 128], F32)
    o_dseg = k.dout("d_seg", [128, 8], F32)
    o_kh = k.dout("kT_halo", [4, 128, 128], halo_dt)
    o_vh = k.dout("v_halo", [128, 256], halo_dt)
    o_hh = k.dout("h_halo", [8, 128, 2], F32)
    idb = k.sb("idb", [128, 128], BF16)
    k.dma("sp", idb[:], ident[:, :], "id", w=["idb"])
    lbs = lower_bound(k, c)
    ones = k.sb("ones", [128, T], F32)
    k.op("pool", lambda e: e.memset(ones[:], 1.0), w=["ones"])
    fb = k.sb("fb", [128, T], F32); gb = k.sb("gb", [128, T], F32); bs = k.sb("bs", [128, T], F32)
    ksg = k.sb("ksg", [128, T], BF16)
    vh = k.sb("vh", [128, NB, 128], BF16)
    ksT = k.sb("ksT", [128, NB, 128], BF16)
    dsg = k.sb("dsg", [128, 8], F32)
    sst = [k.sb(f"sst{i}", [128, 128], F32) for i in range(2)]
    ptr = [k.ps(f"ptr{i}", [128, 4, 128], BF16) for i in range(1)]
    for h in range(8):
        forget_gate(k, c, h, lbs, fb, "fb")
        k.op("act", lambda e: e.activation(out=gb[:], in_=fb[:], func=AF.Ln), r=["fb"], w=["gb"])
        k.op("dve", lambda e: e.tensor_tensor_scan(out=bs[:], data0=ones[:], data1=gb[:], initial=0.0, op0=ALU.mult, op1=ALU.add), r=["ones", "gb"], w=["bs"])
        k.op("act", lambda e, h=h: e.activation(out=dsg[:, h:h + 1], in_=bs[:, T - 1:T], func=AF.Exp), r=["bs"], w=["dsg"])
        k.op("dve", lambda e: e.tensor_scalar(out=gb[:], in0=bs[:], scalar1=-1.0, scalar2=bs[:, T - 1:T], op0=ALU.mult, op1=ALU.add), r=["bs"], w=["gb"])
        k.op("act", lambda e: e.activation(out=gb[:], in_=gb[:], func=AF.Exp), r=["gb"], w=["gb"])
        k.op("dve", lambda e: e.tensor_scalar(out=fb[:], in0=fb[:], scalar1=-1.0, scalar2=1.0, op0=ALU.mult, op1=ALU.add), r=["fb"], w=["fb"])
        k.op("dve", lambda e: e.tensor_tensor(out=ksg[:], in0=fb[:], in1=gb[:], op=ALU.mult), r=["fb", "gb"], w=["ksg"])
        wt, wtok = c.slab(C_AI + h)
        for g4 in range(NB // 4 if NB >= 4 else 1):
            nb4 = min(4, NB)
            P, ptok = c.bank()
            for j in range(nb4):
                c.tm(wt, wtok, (g4 * 4 + j) * 128, 128, P, ptok, j * 128)
            k.op("act", lambda e, P=P, g4=g4, nb4=nb4: e.copy(out=vh[:, g4 * 4:g4 * 4 + nb4, :], in_=P[:, 0:nb4 * 128]), r=[ptok], w=["vh"])
        for g4 in range(NB // 4 if NB >= 4 else 1):
            nb4 = min(4, NB)
            for j in range(nb4):
                tb = g4 * 4 + j
                k.op("pe", lambda e, j=j, tb=tb: e.transpose(ptr[0][:, j, :], ksg[:, tb * 128:(tb + 1) * 128], idb[:]), r=["ksg", "idb"], w=["ptr0"], inc=(j == nb4 - 1))
            k.op("dve", lambda e, g4=g4, nb4=nb4: e.tensor_copy(out=ksT[:, g4 * 4:g4 * 4 + nb4, :], in_=ptr[0][:, 0:nb4, :]), r=["ptr0"], w=["ksT"])
        P, ptok = c.bank()
        for tb in range(NB):
            k.op("pe", lambda e, P=P, tb=tb: e.matmul(P[:, 0:128], ksT[:, tb, :], vh[:, tb, :], start=(tb == 0), stop=(tb == NB - 1)), r=["ksT", "vh"], w=[ptok], inc=(tb == NB - 1))
        s = h % 2
        k.op("act", lambda e, P=P, s=s: e.copy(out=sst[s][:], in_=P[:, 0:128]), r=[ptok], w=[f"sst{s}"])
        k.dma("sp", o_sloc[h, :, :], sst[s][:], f"sst{s}", r=[f"sst{s}"])
    k.dma("sp", o_dseg[:, :], dsg[:], "dsg", r=["dsg"])
    t0 = T - 128
    kst = k.sb("kst", [128, 4, 128], halo_dt)
    for kv in range(4):
        wt, wtok = c.slab(kv, src=wkdup)
        P, ptok = c.fm(wt, wtok, t0, 128)
        k.op("act", lambda e, P=P, kv=kv: e.copy(out=kst[:, kv, :], in_=P[:, 0:128]), r=[ptok], w=["kst"])
    k.dma("sp", o_kh.rearrange("a p t -> p a t"), kst[:], "kst", r=["kst"])
    vst = k.sb("vst", [128, 256], halo_dt)
    P, ptok = c.bank()
    for j in range(2):
        wt, wtok = c.slab(C_BV + j)
        c.tm(wt, wtok, t0, 128, P, ptok, j * 128)
    k.op("act", lambda e, P=P: e.copy(out=vst[:], in_=P[:, 0:256]), r=[ptok], w=["vst"])
    k.dma("sp", o_vh[:, :], vst[:], "vst", r=["vst"])
    hst = k.sb("hst", [128, 8, 2], F32)
    ctmp = k.sb("ctmp", [128, 2], F32)
    for j in range(8):
        wt, wtok = c.slab(C_CC + j)
        P1, p1tok = c.fm(wt, wtok, t0, 128)
        wt2, wtok2 = c.slab(C_CX + j)
        P2, p2tok = c.fm(wt2, wtok2, t0, 128)
        k.op("act", lambda e, P1=P1: e.copy(out=ctmp[:], in_=P1[:, 126:128]), r=[p1tok], w=["ctmp"])
        k.op("dve", lambda e, P2=P2, j=j: e.tensor_tensor(out=hst[:, j, :], in0=ctmp[:], in1=P2[:, 126:128], op=ALU.mult), r=["ctmp", p2tok], w=["hst"])
    k.dma("sp", o_hh.rearrange("a p t -> p a t"), hst[:], "hst", r=["hst"])
    return k.finish() if own else None


def build_a(T, full=True, k=None):
    own = k is None
    k = K() if own else k
    c = Ctx(k, T, npsum=3, slabs=None if full else SL_A)
    NCH = T // 64; TG = min(512, T); NG = T // TG; CPG = TG // 64
    ident = k.din("ident", [128, 128], BF16)
    tri_d = k.din("tri", [64, 64], I32)
    sprev_d = k.din("s_prev", [3, 8, 128, 128], F32)
    dprev_d = k.din("d_prev", [3, 128, 8], F32)
    ng_d = k.din("norm_g", [128, 8], F32)
    o_a = k.dout("o_a", [8, 128, T], BF16)
    idb = k.sb("idb", [128, 128], BF16); tri = k.sb("tri", [64, 64], I32)
    k.dma("sp", idb[:], ident[:, :], "id", w=["idb"])
    k.dma("sp", tri[:], tri_d[:, :], "tri", w=["tri"])
    sprev = k.sb("sprev", [128, 3, 8, 128], F32); dprev = k.sb("dprev", [128, 3, 8], F32); ng = k.sb("ng", [128, 8], F32)
    for j in range(3):
        k.dma("sp", sprev[:, j, :, :], sprev_d[j].rearrange("h k v -> k h v"), "sprev", w=["sprev"])
        k.dma("sp", dprev[:, j, :], dprev_d[j, :, :], "dprev", w=["dprev"])
    k.dma("sp", ng[:], ng_d[:, :], "ng", w=["ng"])
    lbs = lower_bound(k, c)
    m01 = k.sb("m01", [128, T], F32)
    k.op("pool", lambda e: e.memset(m01[:], 1.0), w=["m01"])
    k.op("pool", lambda e: e.memset(m01[:].rearrange("p (c s) -> p c s", s=64)[:, :, 0:1], 0.0), r=["m01"], w=["m01"])
    onesb = k.sb("onesb", [128, 128], BF16)
    k.op("pool", lambda e: e.memset(onesb[:], 1.0), w=["onesb"])
    Sf = k.sb("Sf", [128, 8, 128], F32)
    k.op("dve", lambda e: e.tensor_copy(out=Sf[:], in_=sprev[:, 0, :, :]), r=["sprev"], w=["Sf"])
    for j in (1, 2):
        for h in range(8):
            k.op("dve", lambda e, j=j, h=h: e.scalar_tensor_tensor(out=Sf[:, h, :], in0=Sf[:, h, :], scalar=dprev[:, j, h:h + 1], in1=sprev[:, j, h, :], op0=ALU.mult, op1=ALU.add),
                 r=["Sf", "dprev", "sprev"], w=["Sf"])
    qs = k.sb("qs", [128, T], F32); fb = k.sb("fb", [128, T], F32); bb = k.sb("bb", [128, T], F32)
    t1 = k.sb("t1", [128, T], F32); t2 = k.sb("t2", [128, T], F32)
    qt = k.sb("qt", [128, T], BF16); qe = k.sb("qe", [128, T], BF16); kt = k.sb("kt", [128, T], BF16); kte = k.sb("kte", [128, T], BF16)
    dec = k.sb("dec", [128, NCH], F32)
    vh = k.sb("vh", [64, NCH, 128], BF16); kteT = k.sb("kteT", [64, NCH, 128], BF16)
    Sb = [k.sb(f"Sb{i}", [128, 128], BF16) for i in range(2)]
    ptm = [k.sb(f"ptm{i}", [64, 64], BF16) for i in range(2)]
    osq = k.sb("osq", [128, TG], BF16); rs = k.sb("rs", [128, TG], F32)
    oast = [k.sb(f"oast{i}", [128, T], BF16) for i in range(2)]
    po = [k.ps(f"po{i}", [128, 512], F32) for i in range(2)]
    pss = k.ps("pss", [128, 512], F32)
    pmisc = k.ps("pmisc", [128, 512], F32)
    ptr = k.ps("ptr", [64, 4, 128], BF16)
    for i in range(2):
        k.op("pool", lambda e, i=i: e.memset(ptm[i][:], 0.0), w=[f"ptm{i}"])
    b3 = lambda ap: ap.rearrange("p (c s) -> p c s", s=64)
    nsb = 0; npt = 0
    for h in range(8):
        wt, wtok = c.slab(C_AQ + h)
        for tg in range(NG):
            P, ptok = c.fm(wt, wtok, tg * TG, TG)
            k.op("act", lambda e, P=P, tg=tg: e.activation(out=qs[:, tg * TG:(tg + 1) * TG], in_=P[:, 0:TG], func=AF.Silu), r=[ptok], w=["qs"])
        forget_gate(k, c, h, lbs, fb, "fb")
        k.op("act", lambda e: e.activation(out=t1[:], in_=fb[:], func=AF.Ln), r=["fb"], w=["t1"])
        k.op("dve", lambda e: e.tensor_tensor_scan(out=bb[:], data0=m01[:], data1=t1[:], initial=0.0, op0=ALU.mult, op1=ALU.add), r=["m01", "t1"], w=["bb"])
        k.op("dve", lambda e: e.tensor_scalar(out=fb[:], in0=fb[:], scalar1=-1.0, scalar2=1.0, op0=ALU.mult, op1=ALU.add), r=["fb"], w=["fb"])
        k.op("act", lambda e: e.activation(out=dec[:], in_=b3(bb[:])[:, :, 63], func=AF.Exp), r=["bb"], w=["dec"])
        k.op("dve", lambda e: e.tensor_tensor(out=b3(t1[:]), in0=b3(bb[:]), in1=b3(bb[:])[:, :, 31:32].broadcast_to([128, NCH, 64]), op=ALU.subtract), r=["bb"], w=["t1"])
        k.op("act", lambda e: e.activation(out=t2[:], in_=t1[:], func=AF.Exp), r=["t1"], w=["t2"])
        k.op("dve", lambda e: e.scalar_tensor_tensor(out=qt[:], in0=qs[:], scalar=128 ** -0.5, in1=t2[:], op0=ALU.mult, op1=ALU.mult), r=["qs", "t2"], w=["qt"])
        k.op("act", lambda e: e.activation(out=t2[:], in_=t1[:], func=AF.Exp, scale=-1.0), r=["t1"], w=["t2"])
        k.op("dve", lambda e: e.tensor_tensor(out=kt[:], in0=fb[:], in1=t2[:], op=ALU.mult), r=["fb", "t2"], w=["kt"])
        k.op("dve", lambda e: e.tensor_tensor(out=b3(t1[:]), in0=b3(bb[:])[:, :, 63:64].broadcast_to([128, NCH, 64]), in1=b3(bb[:]), op=ALU.subtract), r=["bb"], w=["t1"])
        k.op("act", lambda e: e.activation(out=t2[:], in_=t1[:], func=AF.Exp), r=["t1"], w=["t2"])
        k.op("dve", lambda e: e.tensor_tensor(out=kte[:], in0=fb[:], in1=t2[:], op=ALU.mult), r=["fb", "t2"], w=["kte"])
        k.op("act", lambda e: e.activation(out=t2[:], in_=bb[:], func=AF.Exp), r=["bb"], w=["t2"])
        k.op("dve", lambda e: e.scalar_tensor_tensor(out=qe[:], in0=qs[:], scalar=128 ** -0.5, in1=t2[:], op0=ALU.mult, op1=ALU.mult), r=["qs", "t2"], w=["qe"])
        wt, wtok = c.slab(C_AI + h)
        for c4 in range(NCH // 4):
            P, ptok = c.bank()
            for j in range(4):
                c.tm(wt, wtok, (c4 * 4 + j) * 64, 64, P, ptok, j * 128)
            k.op("act", lambda e, P=P, c4=c4: e.copy(out=vh[:, c4 * 4:c4 * 4 + 4, :], in_=P[0:64, :]), r=[ptok], w=["vh"])
        for c4 in range(NCH // 4):
            for j in range(4):
                cc = c4 * 4 + j
                k.op("pe", lambda e, j=j, cc=cc: e.transpose(ptr[:, j, :], kte[:, cc * 64:(cc + 1) * 64], idb[:]), r=["kte", "idb"], w=["ptr"], inc=(j == 3))
            k.op("dve", lambda e, c4=c4: e.tensor_copy(out=kteT[:, c4 * 4:c4 * 4 + 4, :], in_=ptr[:]), r=["ptr"], w=["kteT"])
        sb_cur = nsb % 2; nsb += 1
        k.op("act", lambda e, h=h, s=sb_cur: e.copy(out=Sb[s][:], in_=Sf[:, h, :]), r=["Sf"], w=[f"Sb{sb_cur}"])
        for cc in range(NCH):
            g = cc // CPG; j = cc % CPG; pg = g % 2
            s = npt % 2; npt += 1
            cs = slice(cc * 64, (cc + 1) * 64)
            k.op("pe", lambda e, s=s, cs=cs: e.matmul(pmisc[0:64, 256 + s * 64:320 + s * 64], kt[:, cs], qt[:, cs], start=True, stop=True), r=["kt", "qt"], w=[f"ppt{s}"])
            k.op("dve", lambda e, s=s: e.copy_predicated(out=ptm[s][:], mask=tri[:], data=pmisc[0:64, 256 + s * 64:320 + s * 64]), r=[f"ppt{s}", "tri"], w=[f"ptm{s}"])
            k.op("pe", lambda e, s=s, cc=cc, pg=pg, j=j: e.matmul(po[pg][:, j * 64:(j + 1) * 64], vh[:, cc, :], ptm[s][:], start=True, stop=False), r=["vh", f"ptm{s}"], w=[f"po{pg}"], inc=False)
            k.op("pe", lambda e, sb_cur=sb_cur, cs=cs, pg=pg, j=j: e.matmul(po[pg][:, j * 64:(j + 1) * 64], Sb[sb_cur][:], qe[:, cs], start=False, stop=True), r=[f"Sb{sb_cur}", "qe"], w=[f"po{pg}"])
            k.op("pe", lambda e, s=s, cc=cc: e.matmul(pmisc[:, s * 128:(s + 1) * 128], kteT[:, cc, :], vh[:, cc, :], start=True, stop=True), r=["kteT", "vh"], w=[f"pkv{s}"])
            k.op("dve", lambda e, h=h, cc=cc, s=s: e.scalar_tensor_tensor(out=Sf[:, h, :], in0=Sf[:, h, :], scalar=dec[:, cc:cc + 1], in1=pmisc[:, s * 128:(s + 1) * 128], op0=ALU.mult, op1=ALU.add), r=["Sf", "dec", f"pkv{s}"], w=["Sf"])
            sb_cur = nsb % 2; nsb += 1
            k.op("act", lambda e, h=h, s2=sb_cur: e.copy(out=Sb[s2][:], in_=Sf[:, h, :]), r=["Sf"], w=[f"Sb{sb_cur}"])
            if j == CPG - 1:
                gs = slice(g * TG, (g + 1) * TG)
                k.op("act", lambda e, pg=pg: e.activation(out=osq[:], in_=po[pg][:, 0:TG], func=AF.Square), r=[f"po{pg}"], w=["osq"])
                k.op("pe", lambda e: e.matmul(pss[:, 0:TG], onesb[:], osq[:], start=True, stop=True), r=["onesb", "osq"], w=["pss"])
                k.op("act", lambda e: e.activation(out=rs[:], in_=pss[:, 0:TG], func=AF.Sqrt, scale=1.0 / 128, bias=RMS_EPS), r=["pss"], w=["rs"])
                k.op("dve", lambda e: e.reciprocal(out=rs[:], in_=rs[:]), r=["rs"], w=["rs"])
                k.op("dve", lambda e, pg=pg, h=h, gs=gs: e.scalar_tensor_tensor(out=t1[:, gs], in0=po[pg][:, 0:TG], scalar=ng[:, h:h + 1], in1=rs[:], op0=ALU.mult, op1=ALU.mult), r=[f"po{pg}", "ng", "rs"], w=["t1"])
        wt, wtok = c.slab(C_AG + h)
        os_ = h % 2
        for tg in range(NG):
            gs = slice(tg * TG, (tg + 1) * TG)
            P, ptok = c.fm(wt, wtok, tg * TG, TG)
            k.op("act", lambda e, P=P, gs=gs: e.activation(out=t2[:, gs], in_=P[:, 0:TG], func=AF.Silu), r=[ptok], w=["t2"])
            k.op("pool", lambda e, gs=gs, os_=os_: e.tensor_tensor(out=oast[os_][:, gs], in0=t1[:, gs], in1=t2[:, gs], op=ALU.mult), r=["t1", "t2"], w=[f"oast{os_}"])
        k.dma("sp", o_a[h, :, :], oast[os_][:], f"oast{os_}", r=[f"oast{os_}"])
    if own:
        o_S = k.dout("S_end", [128, 8, 128], F32)
        k.dma("sp", o_S[:, :, :], Sf[:], "Sf_out", r=["Sf"])
    return k.finish() if own else None


def build_b(T, full=True, k=None):
    own = k is None
    k = K() if own else k
    c = Ctx(k, T, npsum=3, slabs=None if full else SL_B)
    NB = T // 128; TG = min(512, T); NG = T // TG
    ident = k.din("ident", [128, 128], BF16)
    kh_d = k.din("kT_halo", [4, 128, 128], BF16)
    vh_d = k.din("v_halo", [128, 256], BF16)
    bias_d = k.din("bias", [16, 128, 256], F32)
    nm_d = k.din("negmask", [128, 256], F32)
    fn_d = k.din("firstneg", [128, 1], F32)
    sk_d = k.din("sinks", [128, 16], F32)
    o_b = k.dout("o_b", [8, 128, T], BF16)
    idb = k.sb("idb", [128, 128], BF16)
    k.dma("sp", idb[:], ident[:, :], "id", w=["idb"])
    bm = k.sb("bm", [128, 16, 256], F32); nm = k.sb("nm", [128, 256], F32); fneg = k.sb("fneg", [128, 1], F32); sinkb = k.sb("sinkb", [128, 16], F32)
    for hq in range(16):
        k.dma("sp", bm[:, hq, :], bias_d[hq, :, :], "bm", w=["bm"])
    k.dma("sp", nm[:], nm_d[:, :], "nm", w=["nm"])
    k.dma("sp", fneg[:], fn_d[:, :], "fneg", w=["fneg"])
    k.dma("sp", sinkb[:], sk_d[:, :], "sinkb", w=["sinkb"])
    k.op("dve", lambda e: e.tensor_tensor(out=bm[:], in0=bm[:], in1=nm[:].unsqueeze(1).broadcast_to([128, 16, 256]), op=ALU.add), r=["bm", "nm"], w=["bm"])
    kT2 = k.sb("kT2", [64, 4, 128 + T], BF16)
    vtok = k.sb("vtok", [128, NB + 1, 256], BF16)
    k.dma("sp", kT2[:, :, 0:128], kh_d[:, 0:64, :].rearrange("a p t -> p a t"), "kT2h", w=["kT2h"])
    k.dma("sp", vtok[:, 0, :], vh_d[:, :], "vtokh", w=["vtokh"])
    for kv in range(4):
        if kv % 2 == 0:
            wt, wtok = c.slab(C_BK + kv // 2)
        for tg in range(NG):
            P, ptok = c.fm(wt, wtok, tg * TG, TG, (kv % 2) * 64, (kv % 2) * 64 + 64)
            k.op("act", lambda e, P=P, kv=kv, tg=tg: e.copy(out=kT2[:, kv, 128 + tg * TG:128 + (tg + 1) * TG], in_=P[0:64, 0:TG]), r=[ptok], w=["kT2"])
    wv = [c.slab(C_BV + j) for j in range(2)]
    for t2 in range(NB // 2):
        P, ptok = c.bank()
        for blk in range(2):
            for j in range(2):
                c.tm(wv[j][0], wv[j][1], (t2 * 2 + blk) * 128, 128, P, ptok, blk * 256 + j * 128)
        k.op("act", lambda e, P=P, t2=t2: e.copy(out=vtok[:, 1 + t2 * 2:3 + t2 * 2, :], in_=P[:, :]), r=[ptok], w=["vtok"])
    qT = k.sb("qT", [64, 2, T], BF16); gate = k.sb("gate", [128, T], F32)
    obT = [k.sb(f"obT{i}", [128, T], BF16) for i in range(2)]
    sc = [k.sb(f"sc{i}", [128, 2, 256], F32) for i in range(2)]
    pp = [k.sb(f"pp{i}", [128, 2, 256], BF16) for i in range(2)]
    pT = [k.sb(f"pT{i}", [128, 4, 128], BF16) for i in range(2)]
    on = [k.sb(f"on{i}", [128, 2, 64], BF16) for i in range(2)]
    st = [k.sb(f"st{i}", [128, 8, 2], F32) for i in range(2)]
    psS = [k.ps(f"psS{i}", [128, 2, 256], F32) for i in range(2)]
    ppT = k.ps("ppT", [128, 4, 128], BF16)
    po = k.ps("po", [128, 2, 64], F32)
    poT = k.ps("poT", [128, 128], BF16)
    it = 0
    for qc in range(8):
        kvh = qc // 2
        wt, wtok = c.slab(C_BQ + qc)
        for tg in range(NG):
            for hh in range(2):
                P, ptok = c.fm(wt, wtok, tg * TG, TG, hh * 64, hh * 64 + 64)
                k.op("act", lambda e, P=P, tg=tg, hh=hh: e.mul(out=qT[:, hh, tg * TG:(tg + 1) * TG], in_=P[0:64, 0:TG], mul=0.125), r=[ptok], w=["qT"])
        wt, wtok = c.slab(C_BG + qc)
        for tg in range(NG):
            P, ptok = c.fm(wt, wtok, tg * TG, TG)
            k.op("act", lambda e, P=P, tg=tg: e.activation(out=gate[:, tg * TG:(tg + 1) * TG], in_=P[:, 0:TG], func=AF.Silu), r=[ptok], w=["gate"])
        ost = qc % 2

        def phase1(n, s):
            S_, sc_, pp_, pT_, on_, st_ = psS[s], sc[s], pp[s], pT[s], on[s], st[s]
            for hh in range(2):
                k.op("pe", lambda e, S_=S_, hh=hh, n=n, kvh=kvh: e.matmul(S_[:, hh, :], qT[:, hh, n * 128:(n + 1) * 128], kT2[:, kvh, n * 128:n * 128 + 256], start=True, stop=True),
                     r=["qT", "kT2", "kT2h"], w=[f"psS{s}"], inc=(hh == 1))
            k.op("dve", lambda e, S_=S_, sc_=sc_, qc=qc: e.tensor_tensor(out=sc_[:], in0=S_[:], in1=bm[:, 2 * qc:2 * qc + 2, :], op=ALU.add), r=[f"psS{s}", "bm"], w=[f"sc{s}"])
            if n == 0:
                k.op("dve", lambda e, sc_=sc_: e.tensor_scalar(out=sc_[:, :, 0:128], in0=sc_[:, :, 0:128], scalar1=fneg[:, 0:1], scalar2=None, op0=ALU.add), r=[f"sc{s}", "fneg"], w=[f"sc{s}"])
            k.op("dve", lambda e, sc_=sc_, st_=st_: e.tensor_reduce(out=st_[:, 0, :], in_=sc_[:], axis=AX.X, op=ALU.max), r=[f"sc{s}"], w=[f"st{s}"])
            k.op("dve", lambda e, st_=st_, qc=qc: e.tensor_tensor(out=st_[:, 0, :], in0=st_[:, 0, :], in1=sinkb[:, 2 * qc:2 * qc + 2], op=ALU.max), r=[f"st{s}", "sinkb"], w=[f"st{s}"])
            k.op("dve", lambda e, st_=st_: e.tensor_scalar(out=st_[:, 1, :], in0=st_[:, 0, :], scalar1=-1.0, scalar2=None, op0=ALU.mult), r=[f"st{s}"], w=[f"st{s}"])
            k.op("dve", lambda e, st_=st_, qc=qc: e.tensor_tensor(out=st_[:, 3, :], in0=st_[:, 1, :], in1=sinkb[:, 2 * qc:2 * qc + 2], op=ALU.add), r=[f"st{s}", "sinkb"], w=[f"st{s}"])
            for hh in range(2):
                k.op("act", lambda e, sc_=sc_, pp_=pp_, st_=st_, hh=hh: e.activation(out=pp_[:, hh, :], in_=sc_[:, hh, :], func=AF.Exp, bias=st_[:, 1, hh:hh + 1], accum_out=st_[:, 2, hh:hh + 1]),
                     r=[f"sc{s}", f"st{s}"], w=[f"pp{s}", f"st{s}"])
            k.op("act", lambda e, st_=st_: e.activation(out=st_[:, 3, :], in_=st_[:, 3, :], func=AF.Exp), r=[f"st{s}"], w=[f"st{s}"])
            k.op("dve", lambda e, st_=st_: e.tensor_tensor(out=st_[:, 4, :], in0=st_[:, 2, :], in1=st_[:, 3, :], op=ALU.add), r=[f"st{s}"], w=[f"st{s}"])
            k.op("dve", lambda e, st_=st_: e.reciprocal(out=st_[:, 4, :], in_=st_[:, 4, :]), r=[f"st{s}"], w=[f"st{s}"])

        def phase2(n, s):
            S_, sc_, pp_, pT_, on_, st_ = psS[s], sc[s], pp[s], pT[s], on[s], st[s]
            for hh in range(2):
                for kb in range(2):
                    k.op("pe", lambda e, pp_=pp_, hh=hh, kb=kb: e.transpose(ppT[:, hh * 2 + kb, :], pp_[:, hh, kb * 128:(kb + 1) * 128], idb[:]), r=[f"pp{s}", "idb"], w=["ppT"], inc=(hh == 1 and kb == 1))
            k.op("act", lambda e, pT_=pT_: e.copy(out=pT_[:], in_=ppT[:]), r=["ppT"], w=[f"pT{s}"])
            for hh in range(2):
                for kb in range(2):
                    k.op("pe", lambda e, pT_=pT_, hh=hh, kb=kb, n=n, kvh=kvh: e.matmul(po[:, hh, :], pT_[:, hh * 2 + kb, :], vtok[:, n + kb, kvh * 64:(kvh + 1) * 64], start=(kb == 0), stop=(kb == 1)),
                         r=[f"pT{s}", "vtok", "vtokh"], w=["po"], inc=(hh == 1 and kb == 1))
            k.op("dve", lambda e, on_=on_, st_=st_: e.tensor_tensor(out=on_[:], in0=po[:], in1=st_[:, 4, :].unsqueeze(2).broadcast_to([128, 2, 64]), op=ALU.mult), r=["po", f"st{s}"], w=[f"on{s}"])
            k.op("pe", lambda e, on_=on_: e.transpose(poT[:], on_[:].rearrange("p a b -> p (a b)"), idb[:]), r=[f"on{s}", "idb"], w=["poT"])
            k.op("dve", lambda e, n=n, ost=ost: e.tensor_tensor(out=obT[ost][:, n * 128:(n + 1) * 128], in0=poT[:], in1=gate[:, n * 128:(n + 1) * 128], op=ALU.mult), r=["poT", "gate"], w=[f"obT{ost}"])

        slots = []
        for n in range(NB):
            slots.append(it % 2); it += 1
        phase1(0, slots[0])
        for n in range(NB):
            if n + 1 < NB:
                phase1(n + 1, slots[n + 1])
            phase2(n, slots[n])
        k.dma("sp", o_b[qc, :, :], obT[ost][:], f"obT{ost}", r=[f"obT{ost}"])
    return k.finish() if own else None


def build_c(T, full=True, k=None):
    own = k is None
    k = K() if own else k
    c = Ctx(k, T, nslab=4, npsum=8, slabs=None if full else SL_C)
    TG = min(512, T); NG = T // TG
    hh_d = k.din("h_halo", [8, 128, 2], F32)
    cw_d = k.din("conv_w", [128, 8, 3], F32)
    o_c = k.dout("o_c", [8, 128, T], BF16)
    cw = k.sb("cw", [128, 8, 3], F32)
    k.dma("sp", cw[:], cw_d[:, :, :], "cw", w=["cw"])
    hb = k.sb("hb", [128, 2 + T], F32); cc_ = k.sb("ccs", [128, TG], F32); y = k.sb("y", [128, TG], F32); gs = k.sb("gs", [128, TG], F32)
    ocT = [k.sb(f"ocT{i}", [128, T], BF16) for i in range(2)]
    for j in range(8):
        k.dma("sp", hb[:, 0:2], hh_d[j, :, :], "hbh", w=["hbh"])
        wb, wbt = c.slab(C_CB + j); wc, wct = c.slab(C_CC + j); wx, wxt = c.slab(C_CX + j); wg, wgt = c.slab(C_CG + j)
        os_ = j % 2
        for tg in range(NG):
            t0 = tg * TG
            Pc, pct = c.fm(wc, wct, t0, TG); Px, pxt = c.fm(wx, wxt, t0, TG); Pb, pbt = c.fm(wb, wbt, t0, TG); Pg, pgt = c.fm(wg, wgt, t0, TG)
            k.op("act", lambda e, Pc=Pc: e.copy(out=cc_[:], in_=Pc[:, 0:TG]), r=[pct], w=["ccs"])
            k.op("dve", lambda e, Px=Px, t0=t0: e.tensor_tensor(out=hb[:, 2 + t0:2 + t0 + TG], in0=cc_[:], in1=Px[:, 0:TG], op=ALU.mult), r=["ccs", pxt], w=["hb"])
            k.op("dve", lambda e, j=j, t0=t0: e.tensor_scalar(out=y[:], in0=hb[:, t0:t0 + TG], scalar1=cw[:, j, 0:1], scalar2=None, op0=ALU.mult), r=["hb", "hbh", "cw"], w=["y"])
            k.op("dve", lambda e, j=j, t0=t0: e.scalar_tensor_tensor(out=y[:], in0=hb[:, t0 + 1:t0 + 1 + TG], scalar=cw[:, j, 1:2], in1=y[:], op0=ALU.mult, op1=ALU.add), r=["hb", "hbh", "cw", "y"], w=["y"])
            k.op("dve", lambda e, j=j, t0=t0: e.scalar_tensor_tensor(out=y[:], in0=hb[:, t0 + 2:t0 + 2 + TG], scalar=cw[:, j, 2:3], in1=y[:], op0=ALU.mult, op1=ALU.add), r=["hb", "cw", "y"], w=["y"])
            k.op("act", lambda e, Pg=Pg: e.activation(out=gs[:], in_=Pg[:, 0:TG], func=AF.Silu), r=[pgt], w=["gs"])
            k.op("dve", lambda e, Pb=Pb: e.tensor_tensor(out=y[:], in0=y[:], in1=Pb[:, 0:TG], op=ALU.mult), r=["y", pbt], w=["y"])
            k.op("pool", lambda e, t0=t0, os_=os_: e.tensor_tensor(out=ocT[os_][:, t0:t0 + TG], in0=y[:], in1=gs[:], op=ALU.mult), r=["y", "gs"], w=[f"ocT{os_}"])
        k.dma("sp", o_c[j, :, :], ocT[os_][:], f"ocT{os_}", r=[f"ocT{os_}"])
    return k.finish() if own else None


def build_d(T, full=True, k=None):
    own = k is None
    k = K() if own else k
    c = Ctx(k, T, nslab=3, npsum=7, load_xT=False, slabs=None if full else SL_D)
    TG = min(512, T); NG = T // TG; NBG = TG // 128
    ident = k.din("ident", [128, 128], BF16)
    x_d = k.din("x", [T, 2048], F32)
    o_d = [k.din(n, [8, 128, T], BF16) for n in ("o_a", "o_b", "o_c")]
    wp_d = [k.din(n, [16, 128, 8 * 128], F32) for n in ("wpa_t", "wpb_t", "wpc_t")]
    wo_d = k.din("wout_t", [8, 128, 16 * 256], F32)
    lng_d = k.din("ln_g", [128, 2048], F32); lnb_d = k.din("ln_b", [128, 2048], F32)
    xn_d = k.dout("x_new", [T, 2048], F32)
    xTn_d = k.dout("xT_new", [16, 128, T], BF16)
    idb = k.sb("idb", [128, 128], BF16)
    k.dma("sp", idb[:], ident[:, :], "id", w=["idb"])
    lng = k.sb("lng", [128, 2048], F32); lnb = k.sb("lnb", [128, 2048], F32)
    k.dma("sp", lng[:], lng_d[:, :], "lng", w=["lng"]); k.dma("sp", lnb[:], lnb_d[:, :], "lnb", w=["lnb"])
    xg = k.sb("xg", [128, 16, TG], BF16)
    ob = [k.sb(f"ob{i}", [128, 8, TG], BF16) for i in range(3)]
    mg = k.sb("mg", [128, 16, TG], BF16)
    wps = [[k.sb(f"wp{i}_{j}", [128, 8, 128], BF16) for j in range(2)] for i in range(3)]
    wos = [k.sb(f"wo{j}", [128, 16, 256], BF16) for j in range(2)]
    z = k.sb("z", [128, 2, 2048], F32)
    sa = k.sb("sa", [128, TG], F32); acc = k.sb("acc", [128, TG], F32); tt = k.sb("tt", [128, TG], F32)
    xbf = k.sb("xbf", [128, 2048], BF16)
    xTn = [k.sb(f"xTn{i}", [128, 16, 128], BF16) for i in range(2)]
    stats = k.sb("stats", [128, 4, 6], F32); mv = k.sb("mv", [128, 4], F32)
    ptr = k.ps("ptr", [128, 4, 128], BF16)
    nwo = 0; nblk = 0
    for tg in range(NG):
        t0 = tg * TG
        for kc in range(16):
            k.dma("sp", xg[:, kc, :], c.xT_d[kc, :, t0:t0 + TG], "xg", w=["xg"])
        for i in range(3):
            k.dma("sp", ob[i][:], o_d[i][:, :, t0:t0 + TG].rearrange("a p t -> p a t"), f"ob{i}", w=[f"ob{i}"])
        for fc in range(16):
            ws = fc % 2
            for i in range(3):
                k.dma("pool", wps[i][ws][:].rearrange("p a b -> p (a b)"), wp_d[i][fc, :, :], f"wp{i}_{ws}", w=[f"wp{i}_{ws}"])
            for i, cm in enumerate((C_MA, C_MB, C_MC)):
                wm, wmt = c.slab(cm + fc)
                Pm, pmt = c.bank()
                for kc in range(16):
                    k.op("pe", lambda e, Pm=Pm, wm=wm, kc=kc: e.matmul(Pm[:, 0:TG], wm[:, kc, :], xg[:, kc, :], start=(kc == 0), stop=(kc == 15)), r=[wmt, "xg"], w=[pmt], inc=(kc == 15))
                Py, pyt = c.bank()
                for kc in range(8):
                    k.op("pe", lambda e, Py=Py, i=i, ws=ws, kc=kc: e.matmul(Py[:, 0:TG], wps[i][ws][:, kc, :], ob[i][:, kc, :], start=(kc == 0), stop=(kc == 7)), r=[f"wp{i}_{ws}", f"ob{i}"], w=[pyt], inc=(kc == 7))
                k.op("act", lambda e, Pm=Pm: e.activation(out=sa[:], in_=Pm[:, 0:TG], func=AF.Sigmoid), r=[pmt], w=["sa"])
                if i == 0:
                    k.op("dve", lambda e, Py=Py: e.tensor_tensor(out=acc[:], in0=sa[:], in1=Py[:, 0:TG], op=ALU.mult), r=["sa", pyt], w=["acc"])
                else:
                    k.op("dve", lambda e, Py=Py: e.tensor_tensor(out=tt[:], in0=sa[:], in1=Py[:, 0:TG], op=ALU.mult), r=["sa", pyt], w=["tt"])
                    if i == 1:
                        k.op("pool", lambda e: e.tensor_tensor(out=acc[:], in0=acc[:], in1=tt[:], op=ALU.add), r=["acc", "tt"], w=["acc"])
                    else:
                        k.op("pool", lambda e, fc=fc: e.tensor_tensor(out=mg[:, fc, :], in0=acc[:], in1=tt[:], op=ALU.add), r=["acc", "tt"], w=["mg"])
        for half in range(NBG // 2):
            tb0 = t0 + half * 256
            k.dma("sp", z[:], x_d[tb0:tb0 + 256, :].rearrange("(b p) f -> p b f", p=128), "z", w=["z"])
            for n in range(8):
                wsl_ = nwo % 2; nwo += 1
                k.dma("pool", wos[wsl_][:].rearrange("p a b -> p (a b)"), wo_d[n, :, :], f"wo{wsl_}", w=[f"wo{wsl_}"])
                for b in range(2):
                    Po, pot = c.bank()
                    tl = half * 256 + b * 128
                    for kc in range(16):
                        k.op("pe", lambda e, Po=Po, kc=kc, tl=tl, wsl_=wsl_: e.matmul(Po[:, 0:256], mg[:, kc, tl:tl + 128], wos[wsl_][:, kc, :], start=(kc == 0), stop=(kc == 15)), r=["mg", f"wo{wsl_}"], w=[pot], inc=(kc == 15))
                    k.op("dve", lambda e, Po=Po, b=b, n=n: e.scalar_tensor_tensor(out=z[:, b, n * 256:(n + 1) * 256], in0=z[:, b, n * 256:(n + 1) * 256], scalar=ALPHA, in1=Po[:, 0:256], op0=ALU.mult, op1=ALU.add), r=["z", pot], w=["z"])
            for b in range(2):
                for q in range(4):
                    k.op("dve", lambda e, b=b, q=q: e.bn_stats(out=stats[:, q, :], in_=z[:, b, q * 512:(q + 1) * 512]), r=["z"], w=["stats"])
                k.op("dve", lambda e: e.bn_aggr(out=mv[:, 0:2], in_=stats[:].rearrange("p a b -> p (a b)")), r=["stats"], w=["mv"])
                k.op("act", lambda e: e.activation(out=mv[:, 2:3], in_=mv[:, 1:2], func=AF.Sqrt, bias=LN_EPS), r=["mv"], w=["mv"])
                k.op("dve", lambda e: e.reciprocal(out=mv[:, 2:3], in_=mv[:, 2:3]), r=["mv"], w=["mv"])
                k.op("dve", lambda e: e.scalar_tensor_tensor(out=mv[:, 3:4], in0=mv[:, 0:1], scalar=-1.0, in1=mv[:, 2:3], op0=ALU.mult, op1=ALU.mult), r=["mv"], w=["mv"])
                k.op("act", lambda e, b=b: e.activation(out=z[:, b, :], in_=z[:, b, :], func=AF.Identity, scale=mv[:, 2:3], bias=mv[:, 3:4]), r=["z", "mv"], w=["z"])
                k.op("dve", lambda e, b=b: e.tensor_tensor(out=z[:, b, :], in0=z[:, b, :], in1=lng[:], op=ALU.mult), r=["z", "lng"], w=["z"])
                k.op("pool", lambda e, b=b: e.tensor_tensor(out=z[:, b, :], in0=z[:, b, :], in1=lnb[:], op=ALU.add), r=["z", "lnb"], w=["z"])
                tb = tb0 + b * 128
                k.dma("sp", xn_d[tb:tb + 128, :], z[:, b, :], "zout", r=["z"])
                k.op("act", lambda e, b=b: e.copy(out=xbf[:], in_=z[:, b, :]), r=["z"], w=["xbf"])
                xs_ = nblk % 2; nblk += 1
                for g4 in range(4):
                    for j in range(4):
                        kc = g4 * 4 + j
                        k.op("pe", lambda e, j=j, kc=kc: e.transpose(ptr[:, j, :], xbf[:, kc * 128:(kc + 1) * 128], idb[:]), r=["xbf", "idb"], w=["ptr"], inc=(j == 3))
                    k.op("dve", lambda e, g4=g4, xs_=xs_: e.tensor_copy(out=xTn[xs_][:, g4 * 4:g4 * 4 + 4, :], in_=ptr[:]), r=["ptr"], w=[f"xTn{xs_}"])
                k.dma("sp", xTn_d[:, :, tb:tb + 128].rearrange("k p t -> p k t"), xTn[xs_][:], f"xTn{xs_}", r=[f"xTn{xs_}"])
    return k.finish() if own else None


def w_in_t(w):
    return np.ascontiguousarray(w.reshape(16, 128, 132, 128).transpose(2, 1, 0, 3)).reshape(132, 128, 2048)

def slab_t(w, ncol):
    K, C = w.shape
    return np.ascontiguousarray(w.reshape(K // 128, 128, C // ncol, ncol).transpose(2, 1, 0, 3)).reshape(C // ncol, 128, (K // 128) * ncol)

def wk_dup_t(w):
    out = []
    for kv in range(4):
        cols = w[:, 5120 + kv * 64: 5120 + (kv + 1) * 64]
        out.append(np.concatenate([cols, cols], axis=1))
    return slab_t(np.concatenate(out, axis=1), 128)

def lb_lay(lb_param, l):
    a = np.ascontiguousarray(lb_param.T.reshape(8, 128, 4).transpose(1, 0, 2))
    sel = np.zeros((128, 8, 4), np.float32)
    for j in range(4):
        if 1 <= j <= l:
            sel[:, :, j] = 1.0
    return sel, a.astype(np.float32)


def t5_bucket_tab():
    import math
    d = np.arange(128)
    logd = np.log(np.maximum(d, 1).astype(np.float32) / 16) / math.log(128 / 16)
    large = np.minimum(16 + (logd * 16).astype(np.int32), 31)
    return np.where(d < 16, d, large)

def band_bias(rel_bias):
    i = np.arange(128)[:, None]; j = np.arange(256)[None, :]
    rel = np.clip(128 + i - j, 0, 127)
    bucket = t5_bucket_tab()[rel]
    return np.ascontiguousarray(np.transpose(rel_bias[bucket], (2, 0, 1))).astype(np.float32)

def neg_mask():
    i = np.arange(128)[:, None]; j = np.arange(256)[None, :]
    rel = 128 + i - j
    return np.where((rel >= 0) & (rel < 128), 0.0, -30000.0).astype(np.float32)


def build_prep(T, k=None):
    own = k is None
    k = K() if own else k
    NB = T // 128
    x = k.din("x", [T, 2048], F32)
    ident = k.din("ident", [128, 128], BF16)
    xT = k.dout("xT", [16, 128, T], BF16)
    idb = k.sb("idb", [128, 128], BF16)
    xb = [k.sb(f"xb{i}", [128, 2048], BF16) for i in range(2)]
    xTs = k.sb("xTs", [128, 16, T], BF16)
    pt = [k.ps(f"pt{i}", [128, 4, 128], BF16) for i in range(2)]
    k.dma("sp", idb[:], ident[:, :], "id", w=["idb"])
    n = 0
    for tb in range(NB):
        s = tb % 2
        k.dma("pool", xb[s][:], x[tb * 128:(tb + 1) * 128, :], f"xb{s}", w=[f"xb{s}"])
        for g in range(4):
            p = n % 2; n += 1
            for j in range(4):
                kc = g * 4 + j
                k.op("pe", lambda e, p=p, j=j, s=s, kc=kc: e.transpose(pt[p][:, j, :], xb[s][:, kc * 128:(kc + 1) * 128], idb[:]),
                     r=[f"xb{s}", "idb"], w=[f"pt{p}"], inc=(j == 3))
            eng = "act" if g % 2 == 0 else "dve"
            if eng == "act":
                k.op("act", lambda e, p=p, g=g, tb=tb: e.copy(out=xTs[:, g * 4:(g + 1) * 4, tb * 128:(tb + 1) * 128], in_=pt[p][:]),
                     r=[f"pt{p}"], w=[("xTs", tb)])
            else:
                k.op("dve", lambda e, p=p, g=g, tb=tb: e.tensor_copy(out=xTs[:, g * 4:(g + 1) * 4, tb * 128:(tb + 1) * 128], in_=pt[p][:]),
                     r=[f"pt{p}"], w=[("xTs", tb)])
    k.dma("sp", xT.rearrange("k p t -> p k t"), xTs[:], "out", r=[("xTs", tb) for tb in range(NB)])
    return k.finish() if own else None


T_CORE = 2048
NCORES = 8
_PROGS = {}


def _prog(name):
    if name not in _PROGS:
        b = {"prep": lambda: build_prep(T_CORE), "p1": lambda: build_p1(T_CORE, False), "a": lambda: build_a(T_CORE, False),
             "b": lambda: build_b(T_CORE, False), "c": lambda: build_c(T_CORE, False), "d": lambda: build_d(T_CORE, False)}[name]
        _PROGS[name] = b()
    return _PROGS[name]


def _launch(name, in_maps):
    res = run_bass_kernel_spmd(_prog(name), in_maps, core_ids=list(range(NCORES)))
    return res.results


def kernel_unfused(x, w_in, w_proj_hgrn, w_proj_attn, w_proj_conv, w_out, lb_param, hgrn_norm_g, attn_sinks, conv_w, rel_bias, ln_g, ln_b):
    f32 = np.float32
    x = np.asarray(x, f32)
    B, S, Dm = x.shape
    T = T_CORE; R = S // T
    assert B * R == NCORES
    ident = np.eye(128, dtype=f32).astype(NPBF)
    tri = (np.arange(64)[:, None] <= np.arange(64)[None, :]).astype(np.int32)
    negmask = neg_mask()
    bias = band_bias(np.asarray(rel_bias, f32))
    segs = [np.ascontiguousarray(x[c // R, (c % R) * T:(c % R + 1) * T, :]) for c in range(NCORES)]
    res = _launch("prep", [{"x": segs[c], "ident": ident} for c in range(NCORES)])
    xT = [np.asarray(res[c]["xT"]) for c in range(NCORES)]
    xcur = segs
    z_s = np.zeros((8, 128, 128), f32); z_d = np.zeros((128, 8), f32)
    z_k = np.zeros((4, 128, 128), NPBF); z_v = np.zeros((128, 256), NPBF); z_h = np.zeros((8, 128, 2), f32)
    for l in range(4):
        wl = np.asarray(w_in[l], f32)
        wt = w_in_t(wl)
        num, den = lb_lay(np.asarray(lb_param, f32), l)
        w_p1 = np.ascontiguousarray(wt[SL_P1]); wkd = wk_dup_t(wl)
        r1 = _launch("p1", [{"xT": xT[c], "w_in_t": w_p1, "wk_dup_t": wkd, "lb_num": num, "lb_den": den, "ident": ident} for c in range(NCORES)])
        pred = lambda c, j: (c - j) if (c % R) - j >= 0 else None
        w_a = np.ascontiguousarray(wt[SL_A])
        ng = np.ascontiguousarray(np.asarray(hgrn_norm_g[l], f32).reshape(8, 128).T)
        ins = []
        for c in range(NCORES):
            ps = [pred(c, 3), pred(c, 2), pred(c, 1)]
            sp = np.stack([np.asarray(r1[p]["s_loc"]) if p is not None else z_s for p in ps])
            dp = np.stack([np.asarray(r1[p]["d_seg"]) if p is not None else z_d for p in ps])
            ins.append({"xT": xT[c], "w_in_t": w_a, "lb_num": num, "lb_den": den, "ident": ident, "tri": tri, "s_prev": sp, "d_prev": dp, "norm_g": ng})
        ra = _launch("a", ins)
        w_b = np.ascontiguousarray(wt[SL_B])
        sinks = np.ascontiguousarray(np.broadcast_to(np.asarray(attn_sinks[l], f32)[None, :], (128, 16)))
        ins = []
        for c in range(NCORES):
            p = pred(c, 1)
            ins.append({"xT": xT[c], "w_in_t": w_b, "ident": ident,
                        "kT_halo": np.asarray(r1[p]["kT_halo"]) if p is not None else z_k,
                        "v_halo": np.asarray(r1[p]["v_halo"]) if p is not None else z_v,
                        "bias": bias, "negmask": negmask, "firstneg": np.full((128, 1), NEG if p is None else 0.0, f32), "sinks": sinks})
        rb = _launch("b", ins)
        w_c = np.ascontiguousarray(wt[SL_C])
        cw = np.ascontiguousarray(np.asarray(conv_w[l], f32).T.reshape(8, 128, 3).transpose(1, 0, 2))
        ins = []
        for c in range(NCORES):
            p = pred(c, 1)
            ins.append({"xT": xT[c], "w_in_t": w_c, "h_halo": np.asarray(r1[p]["h_halo"]) if p is not None else z_h, "conv_w": cw})
        rc = _launch("c", ins)
        w_d = np.ascontiguousarray(wt[SL_D])
        wpa = slab_t(np.asarray(w_proj_hgrn[l], f32), 128); wpb = slab_t(np.asarray(w_proj_attn[l], f32), 128); wpc = slab_t(np.asarray(w_proj_conv[l], f32), 128)
        wo = slab_t(np.asarray(w_out[l], f32), 256)
        g_b = np.ascontiguousarray(np.broadcast_to(np.asarray(ln_g[l], f32)[None], (128, 2048)))
        b_b = np.ascontiguousarray(np.broadcast_to(np.asarray(ln_b[l], f32)[None], (128, 2048)))
        ins = []
        for c in range(NCORES):
            ins.append({"xT": xT[c], "x": xcur[c], "w_in_t": w_d, "ident": ident, "o_a": np.asarray(ra[c]["o_a"]), "o_b": np.asarray(rb[c]["o_b"]), "o_c": np.asarray(rc[c]["o_c"]),
                        "wpa_t": wpa, "wpb_t": wpb, "wpc_t": wpc, "wout_t": wo, "ln_g": g_b, "ln_b": b_b})
        rd = _launch("d", ins)
        xcur = [np.asarray(rd[c]["x_new"]) for c in range(NCORES)]
        xT = [np.asarray(rd[c]["xT_new"]) for c in range(NCORES)]
    out = np.empty((B, S, Dm), f32)
    for c in range(NCORES):
        out[c // R, (c % R) * T:(c % R + 1) * T, :] = xcur[c]
    return out

def build_d1(T, k):
    c = Ctx(k, T, nslab=3, npsum=8)
    TG = 512; NG = T // TG
    o_d = [k.din(n, [8, 128, T], BF16) for n in ("o_a", "o_b", "o_c")]
    wp_d = [k.din(n, [16, 128, 8 * 128], F32) for n in ("wpa_t", "wpb_t", "wpc_t")]
    mg_d = k.dout("mg", [16, 128, T], BF16)
    ob = [k.sb(f"ob{i}", [128, 8, T], BF16) for i in range(3)]
    wps = [k.sb(f"wp{j}", [128, 8, 128], BF16) for j in range(3)]
    sa = [k.sb(f"sa{j}", [128, TG], F32) for j in range(2)]
    tt = [k.sb(f"tt{j}", [128, TG], F32) for j in range(2)]
    acc = k.sb("acc", [128, NG, TG], F32)
    mgst = [k.sb(f"mgst{j}", [128, TG], BF16) for j in range(2)]
    for i in range(3):
        for kc in range(8):
            k.dma("sp", ob[i][:, kc, :], o_d[i][kc, :, :], f"ob{i}", w=[f"ob{i}"])
    nw = 0; it = 0
    for fc in range(16):
        for i, cm in enumerate((C_MA, C_MB, C_MC)):
            wm, wmt = c.slab(cm + fc)
            ws = nw % 3; nw += 1
            k.dma("pool", wps[ws][:].rearrange("p a b -> p (a b)"), wp_d[i][fc, :, :], f"wp{ws}", w=[f"wp{ws}"])
            for tg in range(NG):
                t0 = tg * TG
                Pm, pmt = c.fm(wm, wmt, t0, TG)
                Py, pyt = c.bank()
                for kc in range(8):
                    k.op("pe", lambda e, Py=Py, i=i, ws=ws, kc=kc, t0=t0: e.matmul(Py[:, 0:TG], wps[ws][:, kc, :], ob[i][:, kc, t0:t0 + TG], start=(kc == 0), stop=(kc == 7)), r=[f"wp{ws}", f"ob{i}"], w=[pyt], inc=(kc == 7))
                b = it % 2; it += 1
                k.op("act", lambda e, Pm=Pm, b=b: e.activation(out=sa[b][:], in_=Pm[:, 0:TG], func=AF.Sigmoid), r=[pmt], w=[f"sa{b}"])
                if i == 0:
                    k.op("dve", lambda e, Py=Py, b=b, tg=tg: e.tensor_tensor(out=acc[:, tg, :], in0=sa[b][:], in1=Py[:, 0:TG], op=ALU.mult), r=[f"sa{b}", pyt], w=[("acc", tg)])
                else:
                    k.op("dve", lambda e, Py=Py, b=b: e.tensor_tensor(out=tt[b][:], in0=sa[b][:], in1=Py[:, 0:TG], op=ALU.mult), r=[f"sa{b}", pyt], w=[f"tt{b}"])
                    if i == 1:
                        k.op("pool", lambda e, b=b, tg=tg: e.tensor_tensor(out=acc[:, tg, :], in0=acc[:, tg, :], in1=tt[b][:], op=ALU.add), r=[("acc", tg), f"tt{b}"], w=[("acc", tg)])
                    else:
                        k.op("pool", lambda e, b=b, tg=tg: e.tensor_tensor(out=mgst[b][:], in0=acc[:, tg, :], in1=tt[b][:], op=ALU.add), r=[("acc", tg), f"tt{b}"], w=[f"mgst{b}"])
                        k.dma("sp", mg_d[fc, :, t0:t0 + TG], mgst[b][:], f"mgst{b}", r=[f"mgst{b}"])


def build_d2(T, k):
    c = Ctx(k, T, nslab=1, npsum=7, load_xT=False)
    NB = T // 128
    ident = k.din("ident", [128, 128], BF16)
    x_d = k.din("x", [T, 2048], F32)
    mg_d = k.din("mg", [16, 128, T], BF16)
    wo_d = k.din("wout_t", [8, 128, 16 * 256], F32)
    lng_d = k.din("ln_g", [128, 2048], F32); lnb_d = k.din("ln_b", [128, 2048], F32)
    xn_d = k.dout("x_new", [T, 2048], F32)
    xTn_d = k.dout("xT_new", [16, 128, T], BF16)
    idb = k.sb("idb", [128, 128], BF16)
    k.dma("sp", idb[:], ident[:, :], "id", w=["idb"])
    lng = k.sb("lng", [128, 2048], F32); lnb = k.sb("lnb", [128, 2048], F32)
    k.dma("sp", lng[:], lng_d[:, :], "lng", w=["lng"]); k.dma("sp", lnb[:], lnb_d[:, :], "lnb", w=["lnb"])
    wo = k.sb("wo", [128, 16, 2048], BF16)
    for n in range(8):
        k.dma("pool", wo[:, :, n * 256:(n + 1) * 256], wo_d[n, :, :].rearrange("p (a b) -> p a b", b=256), "wo", w=["wo"])
    mgb = [k.sb(f"mgb{j}", [128, 16, 128], BF16) for j in range(2)]
    z = [k.sb(f"z{j}", [128, 2048], F32) for j in range(2)]
    xbf = k.sb("xbf", [128, 2048], BF16)
    xTn = [k.sb(f"xTn{i}", [128, 16, 128], BF16) for i in range(2)]
    stats = k.sb("stats", [128, 4, 6], F32); mv = k.sb("mv", [128, 4], F32)
    ptr = k.ps("ptr", [128, 4, 128], BF16)
    for blk in range(NB):
        s = blk % 2; tb = blk * 128
        k.dma("sp", mgb[s][:], mg_d[:, :, tb:tb + 128].rearrange("k p t -> p k t"), f"mgb{s}", w=[f"mgb{s}"])
        k.dma("sp", z[s][:], x_d[tb:tb + 128, :], f"zin{s}", w=[f"z{s}"])
        for n4 in range(4):
            Po, pot = c.bank()
            for kc in range(16):
                k.op("pe", lambda e, Po=Po, kc=kc, s=s, n4=n4: e.matmul(Po[:, 0:512], mgb[s][:, kc, :], wo[:, kc, n4 * 512:(n4 + 1) * 512], start=(kc == 0), stop=(kc == 15)), r=[f"mgb{s}", "wo"], w=[pot], inc=(kc == 15))
            k.op("dve", lambda e, Po=Po, s=s, n4=n4: e.scalar_tensor_tensor(out=z[s][:, n4 * 512:(n4 + 1) * 512], in0=z[s][:, n4 * 512:(n4 + 1) * 512], scalar=ALPHA, in1=Po[:, 0:512], op0=ALU.mult, op1=ALU.add), r=[f"z{s}", pot], w=[f"z{s}"])
        for q in range(4):
            k.op("dve", lambda e, s=s, q=q: e.bn_stats(out=stats[:, q, :], in_=z[s][:, q * 512:(q + 1) * 512]), r=[f"z{s}"], w=["stats"])
        k.op("dve", lambda e: e.bn_aggr(out=mv[:, 0:2], in_=stats[:].rearrange("p a b -> p (a b)")), r=["stats"], w=["mv"])
        k.op("act", lambda e: e.activation(out=mv[:, 2:3], in_=mv[:, 1:2], func=AF.Sqrt, bias=LN_EPS), r=["mv"], w=["mv"])
        k.op("dve", lambda e: e.reciprocal(out=mv[:, 2:3], in_=mv[:, 2:3]), r=["mv"], w=["mv"])
        k.op("dve", lambda e: e.scalar_tensor_tensor(out=mv[:, 3:4], in0=mv[:, 0:1], scalar=-1.0, in1=mv[:, 2:3], op0=ALU.mult, op1=ALU.mult), r=["mv"], w=["mv"])
        k.op("act", lambda e, s=s: e.activation(out=z[s][:], in_=z[s][:], func=AF.Identity, scale=mv[:, 2:3], bias=mv[:, 3:4]), r=[f"z{s}", "mv"], w=[f"z{s}"])
        k.op("dve", lambda e, s=s: e.tensor_tensor(out=z[s][:], in0=z[s][:], in1=lng[:], op=ALU.mult), r=[f"z{s}", "lng"], w=[f"z{s}"])
        k.op("pool", lambda e, s=s: e.tensor_tensor(out=z[s][:], in0=z[s][:], in1=lnb[:], op=ALU.add), r=[f"z{s}", "lnb"], w=[f"z{s}"])
        k.dma("sp", xn_d[tb:tb + 128, :], z[s][:], f"zout{s}", r=[f"z{s}"])
        k.op("act", lambda e, s=s: e.copy(out=xbf[:], in_=z[s][:]), r=[f"z{s}"], w=["xbf"])
        for g4 in range(4):
            for j in range(4):
                kc = g4 * 4 + j
                k.op("pe", lambda e, j=j, kc=kc: e.transpose(ptr[:, j, :], xbf[:, kc * 128:(kc + 1) * 128], idb[:]), r=["xbf", "idb"], w=["ptr"], inc=(j == 3))
            k.op("dve", lambda e, g4=g4, s=s: e.tensor_copy(out=xTn[s][:, g4 * 4:g4 * 4 + 4, :], in_=ptr[:]), r=["ptr"], w=[f"xTn{s}"])
        k.dma("sp", xTn_d[:, :, tb:tb + 128].rearrange("k p t -> p k t"), xTn[s][:], f"xTn{s}", r=[f"xTn{s}"])


XW = 1816


def stage_exchange(k, cc_in_t, cc_out_t):
    s_loc = k.din("s_loc", [8, 128, 128], F32); d_seg = k.din("d_seg", [128, 8], F32)
    kh = k.din("kT_halo", [4, 128, 128], F32); vh = k.din("v_halo", [128, 256], F32); hh = k.din("h_halo", [8, 128, 2], F32)
    oh_d = k.din("onehot", [128, 8], F32); sel_d = k.din("selm", [128, 3, 8], F32)
    s_prev = k.dout("s_prev", [3, 8, 128, 128], F32); d_prev = k.dout("d_prev", [3, 128, 8], F32)
    kh_p = k.dout("kT_halo_p", [4, 128, 128], BF16); vh_p = k.dout("v_halo_p", [128, 256], BF16); hh_p = k.dout("h_halo_p", [8, 128, 2], F32)
    cc_in = cc_in_t.ap(); cc_out = cc_out_t.ap()
    pay = k.sb("pay", [128, XW], F32); oh = k.sb("oh", [128, 8], F32); selm = k.sb("selm", [128, 3, 8], F32)
    tmp = [k.sb(f"xtmp{i}", [128, XW], F32) for i in range(2)]
    prev = k.sb("prev", [128, 3, XW], F32)
    hb16 = k.sb("hb16", [128, 768], BF16)
    k.dma("sp", oh[:], oh_d[:, :], "oh", w=["oh"]); k.dma("sp", selm[:], sel_d[:, :, :], "selm", w=["selm"])
    k.dma("sp", pay[:, 0:1024].rearrange("p (h v) -> p h v", v=128), s_loc.rearrange("h k v -> k h v"), "pay", w=["pay"])
    k.dma("sp", pay[:, 1024:1032], d_seg[:, :], "pay", w=["pay"])
    k.dma("sp", pay[:, 1032:1544].rearrange("p (a t) -> p a t", t=128), kh.rearrange("a p t -> p a t"), "pay", w=["pay"])
    k.dma("sp", pay[:, 1544:1800], vh[:, :], "pay", w=["pay"])
    k.dma("sp", pay[:, 1800:1816].rearrange("p (a t) -> p a t", t=2), hh.rearrange("a p t -> p a t"), "pay", w=["pay"])
    for j in range(8):
        t = j % 2
        k.op("dve", lambda e, j=j, t=t: e.tensor_scalar(out=tmp[t][:], in0=pay[:], scalar1=oh[:, j:j + 1], scalar2=None, op0=ALU.mult), r=["pay", "oh"], w=[f"xtmp{t}"])
        k.dma("sp", cc_in[:, j * XW:(j + 1) * XW], tmp[t][:], f"xtmp{t}", r=[f"xtmp{t}"], w=["cc_in"])
    k.cc_allreduce(cc_in_t, cc_out_t, NCORES, r=["cc_in"], w=["cc_out"])
    for slot in range(8):
        t = slot % 2
        k.dma("sp", tmp[t][:], cc_out[:, slot * XW:(slot + 1) * XW], f"xtmp{t}", r=["cc_out"], w=[f"xtmp{t}"])
        for j in range(3):
            if slot == 0:
                k.op("dve", lambda e, j=j, t=t, slot=slot: e.tensor_scalar(out=prev[:, j, :], in0=tmp[t][:], scalar1=selm[:, j, slot:slot + 1], scalar2=None, op0=ALU.mult), r=[f"xtmp{t}", "selm"], w=["prev"])
            else:
                k.op("dve", lambda e, j=j, t=t, slot=slot: e.scalar_tensor_tensor(out=prev[:, j, :], in0=tmp[t][:], scalar=selm[:, j, slot:slot + 1], in1=prev[:, j, :], op0=ALU.mult, op1=ALU.add), r=[f"xtmp{t}", "selm", "prev"], w=["prev"])
    for j in range(3):
        k.dma("sp", s_prev[j].rearrange("h k v -> k h v"), prev[:, j, 0:1024].rearrange("p (h v) -> p h v", v=128), "prevo", r=["prev"])
        k.dma("sp", d_prev[j, :, :], prev[:, j, 1024:1032], "prevo", r=["prev"])
    k.op("act", lambda e: e.copy(out=hb16[:], in_=prev[:, 2, 1032:1800]), r=["prev"], w=["hb16"])
    k.dma("sp", kh_p.rearrange("a p t -> p a t"), hb16[:, 0:512].rearrange("p (a t) -> p a t", t=128), "prevo", r=["hb16"])
    k.dma("sp", vh_p[:, :], hb16[:, 512:768], "prevo", r=["hb16"])
    k.dma("sp", hh_p.rearrange("a p t -> p a t"), prev[:, 2, 1800:1816].rearrange("p (a t) -> p a t", t=2), "prevo", r=["prev"])


ARENA_COLS = 50 * 1024


def build_fused(T):
    k = K(arena_cols=ARENA_COLS)
    nc = k.nc
    X = lambda n, sh, dt: k.xin(n, sh, dt)
    x_in = X("x", [T, 2048], F32)
    win = X("w_in_all", [4, 132, 128, 2048], F32); wkd = X("wk_dup_all", [4, 4, 128, 2048], F32)
    wpa = X("wpa_all", [4, 16, 128, 1024], F32); wpb = X("wpb_all", [4, 16, 128, 1024], F32); wpc = X("wpc_all", [4, 16, 128, 1024], F32)
    wo = X("wout_all", [4, 8, 128, 4096], F32)
    lbn = X("lb_num_all", [4, 128, 8, 4], F32); lbd = X("lb_den_all", [4, 128, 8, 4], F32)
    ngl = X("norm_g_all", [4, 128, 8], F32); skl = X("sinks_all", [4, 128, 16], F32); cwl = X("conv_w_all", [4, 128, 8, 3], F32)
    lng = X("ln_g_all", [4, 128, 2048], F32); lnb = X("ln_b_all", [4, 128, 2048], F32)
    bias = X("bias", [16, 128, 256], F32); negm = X("negmask", [128, 256], F32)
    ident = X("ident", [128, 128], BF16); tri = X("tri", [64, 64], I32)
    fneg = X("firstneg", [128, 1], F32); oneh = X("onehot", [128, 8], F32); selm = X("selm", [128, 3, 8], F32)
    out = k.xout("out", [T, 2048], F32)
    S = lambda n, sh, dt: nc.dram_tensor(n, list(sh), dt)
    xT_s = S("xT_scr", [16, 128, T], BF16).ap(); xres = S("xres", [T, 2048], F32).ap()
    oa = S("oa_scr", [8, 128, T], BF16).ap(); ob = S("ob_scr", [8, 128, T], BF16).ap(); oc = S("oc_scr", [8, 128, T], BF16).ap()
    s_loc = S("s_loc_scr", [8, 128, 128], F32).ap(); d_seg = S("d_seg_scr", [128, 8], F32).ap()
    khf = S("kh_scr", [4, 128, 128], F32).ap(); vhf = S("vh_scr", [128, 256], F32).ap(); hhf = S("hh_scr", [8, 128, 2], F32).ap()
    s_prev = S("s_prev_scr", [3, 8, 128, 128], F32).ap(); d_prev = S("d_prev_scr", [3, 128, 8], F32).ap()
    khp = S("khp_scr", [4, 128, 128], BF16).ap(); vhp = S("vhp_scr", [128, 256], BF16).ap(); hhp = S("hhp_scr", [8, 128, 2], F32).ap()
    cc_in_t = S("cc_in", [128, 8 * XW], F32); cc_out_t = S("cc_out", [128, 8 * XW], F32)
    mg_s = S("mg_scr", [16, 128, T], BF16).ap()

    def stage(fn, io):
        k.stage_begin(); k.io = io
        fn()
        k.barrier()

    stage(lambda: build_prep(T, k=k), {"x": x_in, "ident": ident, "xT": xT_s})
    for l in range(4):
        base = {"xT": xT_s, "w_in_t": win[l], "ident": ident, "lb_num": lbn[l], "lb_den": lbd[l]}
        stage(lambda: build_p1(T, True, k=k, halo_dt=F32), dict(base, wk_dup_t=wkd[l], s_loc=s_loc, d_seg=d_seg, kT_halo=khf, v_halo=vhf, h_halo=hhf))
        stage(lambda: stage_exchange(k, cc_in_t, cc_out_t), {"s_loc": s_loc, "d_seg": d_seg, "kT_halo": khf, "v_halo": vhf, "h_halo": hhf, "onehot": oneh, "selm": selm,
                                                              "s_prev": s_prev, "d_prev": d_prev, "kT_halo_p": khp, "v_halo_p": vhp, "h_halo_p": hhp})
        stage(lambda: build_a(T, True, k=k), dict(base, tri=tri, s_prev=s_prev, d_prev=d_prev, norm_g=ngl[l], o_a=oa))
        stage(lambda: build_b(T, True, k=k), dict(base, kT_halo=khp, v_halo=vhp, bias=bias, negmask=negm, firstneg=fneg, sinks=skl[l], o_b=ob))
        stage(lambda: build_c(T, True, k=k), dict(base, h_halo=hhp, conv_w=cwl[l], o_c=oc))
        stage(lambda: build_d1(T, k), dict(base, o_a=oa, o_b=ob, o_c=oc, wpa_t=wpa[l], wpb_t=wpb[l], wpc_t=wpc[l], mg=mg_s))
        stage(lambda: build_d2(T, k), dict(base, x=(x_in if l == 0 else xres), mg=mg_s, wout_t=wo[l], ln_g=lng[l], ln_b=lnb[l], x_new=(out if l == 3 else xres), xT_new=xT_s))
    k.io = None
    return k.finish()


_FUSED = {}


def kernel(x, w_in, w_proj_hgrn, w_proj_attn, w_proj_conv, w_out, lb_param, hgrn_norm_g, attn_sinks, conv_w, rel_bias, ln_g, ln_b):
    f32 = np.float32
    x = np.asarray(x, f32)
    B, S, Dm = x.shape
    T = T_CORE; R = S // T
    assert B * R == NCORES
    if "nc" not in _FUSED:
        _FUSED["nc"] = build_fused(T)
    A = lambda a: np.asarray(a, f32)
    shared = {
        "w_in_all": np.stack([w_in_t(A(w_in[l])) for l in range(4)]),
        "wk_dup_all": np.stack([wk_dup_t(A(w_in[l])) for l in range(4)]),
        "wpa_all": np.stack([slab_t(A(w_proj_hgrn[l]), 128) for l in range(4)]),
        "wpb_all": np.stack([slab_t(A(w_proj_attn[l]), 128) for l in range(4)]),
        "wpc_all": np.stack([slab_t(A(w_proj_conv[l]), 128) for l in range(4)]),
        "wout_all": np.stack([slab_t(A(w_out[l]), 256) for l in range(4)]),
        "lb_num_all": np.stack([lb_lay(A(lb_param), l)[0] for l in range(4)]),
        "lb_den_all": np.stack([lb_lay(A(lb_param), l)[1] for l in range(4)]),
        "norm_g_all": np.stack([np.ascontiguousarray(A(hgrn_norm_g[l]).reshape(8, 128).T) for l in range(4)]),
        "sinks_all": np.stack([np.ascontiguousarray(np.broadcast_to(A(attn_sinks[l])[None, :], (128, 16))) for l in range(4)]),
        "conv_w_all": np.stack([np.ascontiguousarray(A(conv_w[l]).T.reshape(8, 128, 3).transpose(1, 0, 2)) for l in range(4)]),
        "ln_g_all": np.stack([np.ascontiguousarray(np.broadcast_to(A(ln_g[l])[None], (128, 2048))) for l in range(4)]),
        "ln_b_all": np.stack([np.ascontiguousarray(np.broadcast_to(A(ln_b[l])[None], (128, 2048))) for l in range(4)]),
        "bias": band_bias(A(rel_bias)), "negmask": neg_mask(),
        "ident": np.eye(128, dtype=f32).astype(NPBF), "tri": (np.arange(64)[:, None] <= np.arange(64)[None, :]).astype(np.int32),
    }
    in_maps = []
    for c in range(NCORES):
        r = c % R
        oh = np.zeros((128, 8), f32); oh[:, c] = 1.0
        sel = np.zeros((128, 3, 8), f32)
        for j in range(3):
            dist = 3 - j
            if r - dist >= 0:
                sel[:, j, c - dist] = 1.0
        m = dict(shared)
        m["x"] = np.ascontiguousarray(x[c // R, r * T:(r + 1) * T, :])
        m["firstneg"] = np.full((128, 1), NEG if r == 0 else 0.0, f32)
        m["onehot"] = oh; m["selm"] = sel
        in_maps.append(m)
    res = run_bass_kernel_spmd(_FUSED["nc"], in_maps, core_ids=list(range(NCORES))).results
    out = np.empty((B, S, Dm), f32)
    for c in range(NCORES):
        out[c // R, (c % R) * T:(c % R + 1) * T, :] = np.asarray(res[c]["out"])
    return out
```
